# Optimizing a Trainium2 kernel written in Bass

```python
import math
import jax
import jax.numpy as jnp
from jax import lax
import numpy as np

D_MODEL = 4096
BATCH = 4
SEQ = 4096
DEPTH = 1

CHUNK = 64
QBLOCK = 128
HEAD_DIM = 128
WIDTH_A = D_MODEL // 2
WIDTH_B = D_MODEL - WIDTH_A
MIX_WIDTH = WIDTH_A + WIDTH_B
N_DIFF_HEADS = WIDTH_A // (2 * HEAD_DIM)
N_BAND_HEADS = WIDTH_B // HEAD_DIM
LEFT_CHUNKS = 8
BAND = LEFT_CHUNKS + 1
REL_CLIP = 256
N_REL = 2 * REL_CLIP + 1
ROPE_THETA = 10000.0
MEM_LEN = 256
N_MEM_HEADS = 4
MEM_HEAD_DIM = D_MODEL // N_MEM_HEADS
D_FF = ((8 * D_MODEL // 3 + 255) // 256) * 256
CONV_W = 3
IN_COLS = 3 * WIDTH_A + 3 * WIDTH_B
LAMBDA_STD = 0.1
REL_BIAS_STD = 0.2
EPS = 1e-6

kernel_name = "hybrid_diffattn_chunkband_streaming_layer"


def rms_norm(x, w):
    xf = x.astype(jnp.float32)
    y = xf * lax.rsqrt(jnp.mean(xf * xf, axis=-1, keepdims=True) + EPS)
    return (y * w.astype(jnp.float32)).astype(x.dtype)


def rope_tables(seq, dim, dtype):
    inv = 1.0 / (ROPE_THETA ** (jnp.arange(0, dim, 2, dtype=jnp.float32) / dim))
    ang = jnp.arange(seq, dtype=jnp.float32)[:, None] * inv[None, :]
    ang = jnp.concatenate([ang, ang], axis=-1)
    return jnp.cos(ang).astype(dtype), jnp.sin(ang).astype(dtype)


def apply_rope(t, cos, sin):
    t1, t2 = jnp.split(t, 2, axis=-1)
    return t * cos + jnp.concatenate([-t2, t1], axis=-1) * sin


def diff_attention(q, k, v, lam, subln_w, lambda_init):
    B, H2, S, Dh = q.shape
    H = H2 // 2
    nqb = S // QBLOCK
    scale = Dh ** -0.5
    qb = q.reshape(B, H2, nqb, QBLOCK, Dh).transpose(2, 0, 1, 3, 4)
    key_chunk = jnp.arange(S) // CHUNK

    def block(args):
        qi, i = args
        s = jnp.einsum('bhqd,bhkd->bhqk', qi, k).astype(jnp.float32) * scale
        q_chunk = (i * QBLOCK + jnp.arange(QBLOCK)) // CHUNK
        mask = key_chunk[None, :] <= q_chunk[:, None]
        s = jnp.where(mask[None, None], s, -jnp.inf)
        p = jax.nn.softmax(s, axis=-1).reshape(B, H, 2, QBLOCK, S)
        a = p[:, :, 0] - lam * p[:, :, 1]
        return jnp.einsum('bhqk,bhkd->bhqd', a.astype(v.dtype), v)

    o = lax.map(block, (qb, jnp.arange(nqb)))
    o = o.transpose(1, 2, 0, 3, 4).reshape(B, H, S, 2 * Dh)
    return rms_norm(o, subln_w) * (1.0 - lambda_init)


def chunk_band_attention(q, k, v, rel_bias):
    B, H, S, Dh = q.shape
    nc = S // CHUNK
    scale = Dh ** -0.5
    qc = q.reshape(B, H, nc, CHUNK, Dh)
    pad = ((0, 0), (0, 0), (LEFT_CHUNKS, 0), (0, 0), (0, 0))
    kc = jnp.pad(k.reshape(B, H, nc, CHUNK, Dh), pad)
    vc = jnp.pad(v.reshape(B, H, nc, CHUNK, Dh), pad)
    off = jnp.arange(BAND)[:, None, None]
    rel = (off - LEFT_CHUNKS) * CHUNK + jnp.arange(CHUNK)[None, None, :] - jnp.arange(CHUNK)[None, :, None]
    rel = jnp.clip(rel, -REL_CLIP, REL_CLIP) + REL_CLIP
    bias = rel_bias.astype(jnp.float32)[:, rel]
    bias = bias.transpose(0, 2, 1, 3).reshape(H, CHUNK, BAND * CHUNK)
    valid = (jnp.arange(nc)[:, None] + jnp.arange(BAND)[None, :] - LEFT_CHUNKS) >= 0
    valid = jnp.repeat(valid, CHUNK, axis=1)
    s = jnp.concatenate(
        [jnp.einsum('bhcqd,bhckd->bhcqk', qc, kc[:, :, j:j + nc]) for j in range(BAND)], axis=-1
    ).astype(jnp.float32) * scale
    s = s + bias[None, :, None]
    s = jnp.where(valid[None, None, :, None, :], s, -jnp.inf)
    p = jax.nn.softmax(s, axis=-1).astype(v.dtype).reshape(B, H, nc, CHUNK, BAND, CHUNK)
    o = sum(jnp.einsum('bhcqk,bhckd->bhcqd', p[:, :, :, :, j], vc[:, :, j:j + nc]) for j in range(BAND))
    return o.reshape(B, H, S, Dh)


def parallel_mixer(h, w_in, lq1, lk1, lq2, lk2, subln_w, rel_bias, w_out, cos, sin, lambda_init):
    B, S, _ = h.shape
    proj = h @ w_in
    splits = np.cumsum([WIDTH_A, WIDTH_A, WIDTH_A, WIDTH_B, WIDTH_B]).tolist()
    qa, ka, va, qb, kb, vb = jnp.split(proj, splits, axis=-1)

    def heads(t, n, d):
        return t.reshape(B, S, n, d).transpose(0, 2, 1, 3)

    qa = apply_rope(heads(qa, 2 * N_DIFF_HEADS, HEAD_DIM), cos, sin)
    ka = apply_rope(heads(ka, 2 * N_DIFF_HEADS, HEAD_DIM), cos, sin)
    va = heads(va, N_DIFF_HEADS, 2 * HEAD_DIM)
    f32 = jnp.float32
    lam = (jnp.exp(jnp.sum(lq1.astype(f32) * lk1.astype(f32)))
           - jnp.exp(jnp.sum(lq2.astype(f32) * lk2.astype(f32))) + lambda_init)
    oa = diff_attention(qa, ka, va, lam, subln_w, lambda_init)
    ob = chunk_band_attention(heads(qb, N_BAND_HEADS, HEAD_DIM), heads(kb, N_BAND_HEADS, HEAD_DIM),
                              heads(vb, N_BAND_HEADS, HEAD_DIM), rel_bias)
    oa = oa.transpose(0, 2, 1, 3).reshape(B, S, WIDTH_A)
    ob = ob.transpose(0, 2, 1, 3).reshape(B, S, WIDTH_B)
    return jnp.concatenate([oa, ob], axis=-1) @ w_out


def memory_cross_attention(h, mem_n, w_mq, w_mkv, w_mo):
    B, S, D = h.shape
    M = mem_n.shape[1]
    q = (h @ w_mq).reshape(B, S, N_MEM_HEADS, MEM_HEAD_DIM)
    k, v = jnp.split(mem_n @ w_mkv, 2, axis=-1)
    k = k.reshape(B, M, N_MEM_HEADS, MEM_HEAD_DIM)
    v = v.reshape(B, M, N_MEM_HEADS, MEM_HEAD_DIM)
    s = jnp.einsum('bqhd,bkhd->bhqk', q, k).astype(jnp.float32) * (MEM_HEAD_DIM ** -0.5)
    p = jax.nn.softmax(s, axis=-1).astype(v.dtype)
    o = jnp.einsum('bhqk,bkhd->bqhd', p, v).reshape(B, S, D)
    return o @ w_mo


def conv_glu_ffn(h, w_up, conv_w, conv_b, w_down):
    g, u = jnp.split(h @ w_up, 2, axis=-1)
    g = lax.conv_general_dilated(g, conv_w[:, None, :].astype(g.dtype), window_strides=(1,),
                                 padding=[(CONV_W - 1, 0)], dimension_numbers=('NWC', 'WIO', 'NWC'),
                                 feature_group_count=g.shape[-1]) + conv_b
    return (jax.nn.silu(g) * u) @ w_down


def setup_inputs(seed: int = 0) -> dict:
    key = jax.random.key(seed)
    ks = jax.random.split(key, 24)

    def nrm(k, shape, scale):
        return jax.random.normal(k, shape, jnp.float32) * scale

    def gain(k, shape):
        return 1.0 + 0.02 * jax.random.normal(k, shape, jnp.float32)

    return {
        "x": nrm(ks[0], (BATCH, SEQ, D_MODEL), 1.0),
        "mem": nrm(ks[1], (BATCH, MEM_LEN, D_MODEL), 1.0),
        "norm_mix_w": gain(ks[2], (DEPTH, D_MODEL)),
        "w_in": nrm(ks[3], (DEPTH, D_MODEL, IN_COLS), D_MODEL ** -0.5),
        "lambda_q1": nrm(ks[4], (DEPTH, HEAD_DIM), LAMBDA_STD),
        "lambda_k1": nrm(ks[5], (DEPTH, HEAD_DIM), LAMBDA_STD),
        "lambda_q2": nrm(ks[6], (DEPTH, HEAD_DIM), LAMBDA_STD),
        "lambda_k2": nrm(ks[7], (DEPTH, HEAD_DIM), LAMBDA_STD),
        "subln_w": gain(ks[8], (DEPTH, 2 * HEAD_DIM)),
        "rel_bias": nrm(ks[9], (DEPTH, N_BAND_HEADS, N_REL), REL_BIAS_STD),
        "w_out": nrm(ks[10], (DEPTH, MIX_WIDTH, D_MODEL), MIX_WIDTH ** -0.5),
        "norm_xq_w": gain(ks[11], (DEPTH, D_MODEL)),
        "norm_mem_w": gain(ks[12], (DEPTH, D_MODEL)),
        "w_mq": nrm(ks[13], (DEPTH, D_MODEL, D_MODEL), D_MODEL ** -0.5),
        "w_mkv": nrm(ks[14], (DEPTH, D_MODEL, 2 * D_MODEL), D_MODEL ** -0.5),
        "w_mo": nrm(ks[15], (DEPTH, D_MODEL, D_MODEL), D_MODEL ** -0.5),
        "norm_ffn_w": gain(ks[16], (DEPTH, D_MODEL)),
        "w_up": nrm(ks[17], (DEPTH, D_MODEL, 2 * D_FF), D_MODEL ** -0.5),
        "conv_w": nrm(ks[18], (DEPTH, CONV_W, D_FF), CONV_W ** -0.5),
        "conv_b": nrm(ks[19], (DEPTH, D_FF), 0.02),
        "w_down": nrm(ks[20], (DEPTH, D_FF, D_MODEL), D_FF ** -0.5),
        "final_norm_w": gain(ks[21], (D_MODEL,)),
    }


def reference(x, mem, norm_mix_w, w_in, lambda_q1, lambda_k1, lambda_q2, lambda_k2, subln_w, rel_bias,
              w_out, norm_xq_w, norm_mem_w, w_mq, w_mkv, w_mo, norm_ffn_w, w_up, conv_w, conv_b, w_down,
              final_norm_w):
    S = x.shape[1]
    cos, sin = rope_tables(S, HEAD_DIM, x.dtype)
    for l in range(DEPTH):
        lambda_init = 0.8 - 0.6 * math.exp(-0.3 * l)
        x = x + parallel_mixer(rms_norm(x, norm_mix_w[l]), w_in[l], lambda_q1[l], lambda_k1[l],
                               lambda_q2[l], lambda_k2[l], subln_w[l], rel_bias[l], w_out[l],
                               cos, sin, lambda_init)
        x = x + memory_cross_attention(rms_norm(x, norm_xq_w[l]), rms_norm(mem, norm_mem_w[l]),
                                       w_mq[l], w_mkv[l], w_mo[l])
        x = x + conv_glu_ffn(rms_norm(x, norm_ffn_w[l]), w_up[l], conv_w[l], conv_b[l], w_down[l])
    return rms_norm(x, final_norm_w)
```

```python
import math
import numpy as np
import concourse.bass as bass
import concourse.mybir as mybir
from concourse.bass_utils import run_bass_kernel_spmd

F32 = mybir.dt.float32
BF16 = mybir.dt.bfloat16
AF = mybir.ActivationFunctionType
ALU = mybir.AluOpType
ENGS = ("sp", "act", "pool", "pe", "dve")


class Cfg:
    def __init__(self, D=4096, WA=2048, WB=2048, NMH=4, MEM=256, DFF=11008, TP=16, TO=16, BATCH=4):
        self.D = D; self.KC = D // 128
        self.WA = WA; self.WB = WB
        self.NDH = WA // 256; self.NQH = 2 * self.NDH; self.NBH = WB // 128
        self.NMH = NMH; self.MHD = D // NMH; self.MEM = MEM
        self.DFF = DFF; self.NFB = DFF // 128
        self.TP = TP; self.TO = TO; self.NLT = TP + TO; self.L = self.NLT * 128
        self.QT0 = TP - 1; self.NQT = TO + 1; self.NQ = self.NQT * 128
        self.KB0 = self.QT0 - 4
        self.NO = TO * 128
        self.BATCH = BATCH
        self.INC = 3 * WA + 3 * WB
        assert WA + WB == D and self.KC % 4 == 0 and self.KB0 >= 0


class Res:
    __slots__ = ("w", "r", "rd", "excl")

    def __init__(self):
        self.w = None; self.r = {}; self.rd = []; self.excl = False


class Prog:
    def __init__(self):
        self.ops = []
        self.q = {e: [] for e in ENGS}
        self.res = []
        self.pending = {e: set() for e in ENGS}
        self.last_c = {e: None for e in ENGS}
        self.last_d = {}
        self.bg = set()

    def R(self):
        r = Res(); self.res.append(r); return r

    def Rs(self, n):
        return [self.R() for _ in range(n)]

    def emit(self, eng, fn, reads=(), writes=(), dsem=None):
        idx = len(self.ops)
        deps = set(self.pending[eng]); self.pending[eng] = set()
        for r in reads:
            if r.w is not None: deps.add(r.w)
            if r.excl:
                deps.update(v for k_, v in r.r.items() if k_ != eng)
        for w in writes:
            if w.w is not None: deps.add(w.w)
            deps.update(w.r.values()); deps.update(w.rd)
        isd = dsem is not None
        for r in reads:
            if isd: r.rd.append(idx)
            else: r.r[eng] = idx
        for w in writes:
            w.w = idx; w.r = {}; w.rd = []
        if eng == "pe":
            deps = {d for d in deps if not (self.ops[d][0] == "pe" and self.ops[d][3] is None)}
        self.ops.append([eng, fn, deps, dsem, False])
        self.q[eng].append(idx)
        if isd: self.last_d[dsem] = idx
        else: self.last_c[eng] = idx
        return idx

    def barrier(self):
        toks = set()
        for e in ENGS:
            if self.last_c[e] is not None: toks.add(self.last_c[e])
        toks.update(v for k_, v in self.last_d.items() if k_ not in self.bg)
        for e in ENGS:
            self.pending[e] |= toks
        for r in self.res:
            r.w = None; r.r = {}; r.rd = []

    def join_bg(self, eng):
        self.pending[eng] |= {v for k_, v in self.last_d.items() if k_ in self.bg}
        self.bg = set()

    def replay(self, nc, stack):
        need = set()
        for op in self.ops: need |= op[2]
        for e in ENGS: need |= self.pending[e]
        dnames = sorted({op[3] for op in self.ops if op[3] is not None})
        sem = {}
        for nm in list(ENGS) + ["d_" + d for d in dnames]:
            sem[nm] = stack.enter_context(nc.semaphore("s_" + nm))
        cnt = {k: 0 for k in sem}
        tok = {}
        for idx, op in enumerate(self.ops):
            if op[3] is not None:
                k = "d_" + op[3]; cnt[k] += 16; tok[idx] = (k, cnt[k])
            elif idx in need:
                cnt[op[0]] += 1; tok[idx] = (op[0], cnt[op[0]]); op[4] = True
        self.maxcnt = dict(cnt)
        block = stack.enter_context(nc.Block())

        def run(eng, e):
            waited = {}
            def dowaits(deps):
                best = {}
                for d in deps:
                    k, v = tok[d]
                    if best.get(k, 0) < v: best[k] = v
                for k, v in sorted(best.items()):
                    if waited.get(k, 0) < v:
                        e.wait_ge(sem[k], v); waited[k] = v
            for idx in self.q[eng]:
                _, fn, deps, dsem, sig = self.ops[idx]
                dowaits(deps)
                ins = fn(e)
                if dsem is not None: ins.then_inc(sem["d_" + dsem], 16)
                elif sig: ins.then_inc(sem[eng], 1)
            dowaits(self.pending[eng])

        @block.sync
        def _(e): run("sp", e)

        @block.scalar
        def _(e): run("act", e)

        @block.gpsimd
        def _(e): run("pool", e)

        @block.tensor
        def _(e): run("pe", e)

        @block.vector
        def _(e): run("dve", e)


class Arena:
    def __init__(self, ap, nbytes):
        self.ap = ap; self.n = nbytes; self.off = 0

    def mark(self): return self.off

    def release(self, m): self.off = m

    def _alloc(self, nb):
        o = (self.off + 63) // 64 * 64
        assert o + nb <= self.n, ("arena overflow", o, nb, self.n)
        self.off = o + nb
        return o

    def f32(self, n):
        o = self._alloc(4 * n); return self.ap[:, o // 4: o // 4 + n]

    def bf16(self, n):
        n2 = (n + 1) // 2 * 2
        o = self._alloc(2 * n2); return self.ap[:, o // 4: o // 4 + n2 // 2].bitcast(BF16)[:, 0:n]


def ring(n):
    st = {"i": -1}
    def nxt():
        st["i"] += 1; return st["i"] % n
    return nxt


def build(cfg, debug=False, upto=99):
    c = cfg
    D, KC, L, NQ, NO = c.D, c.KC, c.L, c.NQ, c.NO
    nc = bass.Bass("TRN2", target_bir_lowering=False)
    outs = []

    def din(name, shape, dt=F32):
        return nc.dram_tensor(name, list(shape), dt, kind="ExternalInput").ap()

    def dscr(name, shape, dt):
        if debug: outs.append(name)
        return nc.dram_tensor(name, list(shape), dt, kind=("ExternalOutput" if debug else "Internal")).ap()

    xl = din("xl", [L, D]); memb = din("memb", [c.MEM, D])
    w_in = din("w_in", [D, c.INC]); w_out = din("w_out", [D, D]); w_mq = din("w_mq", [D, D])
    w_mkv = din("w_mkv", [D, 2 * D]); w_mo = din("w_mo", [D, D]); w_up = din("w_up", [D, 2 * c.DFF])
    w_down = din("w_down", [c.DFF, D])
    ncols_d = din("ncols", [128, 4 * KC]); fnw_d = din("fnw", [128, D])
    cosT = din("cosT", [128, L]); sinT = din("sinT", [128, L])
    cst_d = din("cst", [128, 3 * 128])
    lam_d = din("lamp", [128, 4 * 128]); subln_d = din("sublnb", [128, 256]); flag_d = din("flagc", [128, 1])
    cw_d = din("cw", [128, c.NFB * 4]); bt_d = din("bt", [128, c.NBH * 5 * 128])
    out_d = nc.dram_tensor("out", [NO, D], F32, kind="ExternalOutput").ap()

    QTa = dscr("QTa", [c.NQH * 128, NQ], BF16); KTa = dscr("KTa", [c.NQH * 128, L], BF16)
    Va = dscr("Va", [L, c.WA], BF16)
    QTb = dscr("QTb", [c.WB, NQ], BF16); KTb = dscr("KTb", [c.WB, L], BF16); Vb = dscr("Vb", [L, c.WB], BF16)
    ATmix = dscr("ATmix", [D, NQ], BF16)
    X1 = dscr("X1", [NQ, D], F32); X2 = dscr("X2", [NQ, D], F32)
    QmT = dscr("QmT", [D, NQ], BF16); KmT = dscr("KmT", [D, c.MEM], BF16); Vm = dscr("Vm", [c.MEM, D], BF16)
    ATmo = dscr("ATmo", [D, NQ], BF16)
    ACTT = dscr("ACTT", [c.DFF, NO], BF16)
    WDb = dscr("WDb", [c.DFF, D], BF16)

    from contextlib import ExitStack
    stack = ExitStack()
    ARENA_B = 200 * 1024
    arena_t = stack.enter_context(nc.sbuf_tensor("arena", [128, ARENA_B // 4], F32))
    psum = stack.enter_context(nc.psum_tensor("psum", [128, 8, 512], F32))
    A = Arena(arena_t, ARENA_B)
    P = Prog()
    bank = [psum[:, i, :] for i in range(8)]
    Rbank = P.Rs(8)
    for r_ in Rbank: r_.excl = True

    def finish():
        P.barrier()
        P.replay(nc, stack)
        stack.close()
        return nc, outs, P

    def dma(q, out, in_, dsem, reads=(), writes=()):
        P.emit(q, lambda e: e.dma_start(out=out, in_=in_), reads, writes, dsem=dsem)

    def mm(out, lhsT, rhs, start, stop, reads, writes):
        P.emit("pe", lambda e: e.matmul(out, lhsT=lhsT, rhs=rhs, start=start, stop=stop), reads, writes)

    def tr(out, in_, ident, reads, writes):
        P.emit("pe", lambda e: e.transpose(out, in_, ident), reads, writes)

    def act(out, in_, func, reads, writes, scale=None, bias=None, accum_out=None):
        kw = {}
        if scale is not None: kw["scale"] = scale
        if bias is not None: kw["bias"] = bias
        if accum_out is not None: kw["accum_out"] = accum_out
        P.emit("act", lambda e: e.activation(out=out, in_=in_, func=func, **kw), reads, writes)

    def ts(eng, out, in0, s1, s2, op0, op1, reads, writes):
        if op1 is None:
            P.emit(eng, lambda e: e.tensor_scalar(out=out, in0=in0, scalar1=s1, scalar2=None, op0=op0), reads, writes)
        else:
            P.emit(eng, lambda e: e.tensor_scalar(out=out, in0=in0, scalar1=s1, scalar2=s2, op0=op0, op1=op1), reads, writes)

    def tt(eng, out, in0, in1, op, reads, writes):
        P.emit(eng, lambda e: e.tensor_tensor(out=out, in0=in0, in1=in1, op=op), reads, writes)

    def stt(out, in0, scalar, in1, op0, op1, reads, writes):
        P.emit("dve", lambda e: e.scalar_tensor_tensor(out=out, in0=in0, scalar=scalar, in1=in1, op0=op0, op1=op1), reads, writes)

    def recip(out, in_, reads, writes):
        P.emit("dve", lambda e: e.reciprocal(out=out, in_=in_), reads, writes)

    def memset(eng, ap, val, reads, writes):
        P.emit(eng, lambda e: e.memset(ap, val), reads, writes)

    def cpy(eng, out, in_, reads, writes):
        if eng == "act":
            act(out, in_, AF.Copy, reads, writes)
        else:
            P.emit(eng, lambda e: e.tensor_copy(out=out, in_=in_), reads, writes)

    def rstd_chain(rs, ss, inv_n, eps, Rss, Rrs, post=1.0):
        act(rs, ss, AF.Sqrt, [Rss], [Rrs], scale=inv_n / (post * post), bias=epsb(eps / (post * post)))
        recip(rs, rs, [Rrs], [Rrs])

    ident_f = A.f32(128); ones_f = A.f32(128)
    ident_b = A.bf16(128); Rm_b = A.bf16(128); ones_b = A.bf16(128)
    ncols = A.f32(4 * KC); flag = A.f32(1); lamw = A.f32(512); lamc = A.f32(4); neglam = A.f32(1)
    Rc = P.R()
    eps_tiles = {}
    for ev_ in (1e-6, 1e-6 / ((1.0 - (0.8 - 0.6 * math.exp(0.0))) ** 2)):
        t_ = A.f32(1); eps_tiles[ev_] = t_
        memset("dve", t_, ev_, [], [Rc])

    def epsb(v):
        for k_, t_ in eps_tiles.items():
            if abs(k_ - v) <= 1e-12 * max(1.0, abs(v)): return t_
        raise KeyError(v)
    dma("sp", ident_f, cst_d[:, 0:128], "c0", writes=[Rc])
    dma("sp", ones_f, cst_d[:, 256:384], "c0", writes=[Rc])
    dma("sp", ncols, ncols_d, "c0", writes=[Rc])
    dma("sp", flag, flag_d, "c0", writes=[Rc])
    dma("sp", lamw, lam_d, "c0", writes=[Rc])
    dma("pool", ident_b, cst_d[:, 0:128], "c1", writes=[Rc])
    dma("pool", Rm_b, cst_d[:, 128:256], "c1", writes=[Rc])
    dma("pool", ones_b, cst_d[:, 256:384], "c1", writes=[Rc])
    lam_init = 0.8 - 0.6 * math.exp(0.0)
    ljunk = A.f32(128)
    for k in range(2):
        P.emit("dve", (lambda k: lambda e: e.tensor_tensor(out=ljunk, in0=lamw[:, 256 * k:256 * k + 128],
                                                            in1=lamw[:, 256 * k + 128:256 * k + 256], op=ALU.mult))(k),
               [Rc], [Rc])
        P.emit("dve", (lambda k: lambda e: e.tensor_reduce(out=lamc[:, k:k + 1], in_=ljunk, op=ALU.add,
                                                           axis=mybir.AxisListType.X))(k), [Rc], [Rc])
    act(lamc[:, 0:2], lamc[:, 0:2], AF.Exp, [Rc], [Rc])
    tt("dve", lamc[:, 2:3], lamc[:, 1:2], lamc[:, 0:1], ALU.subtract, [Rc], [Rc])
    ts("dve", neglam, lamc[:, 2:3], -lam_init, None, ALU.add, None, [Rc], [Rc])
    P.barrier()
    PM = A.mark()

    def wv(w):
        return w.rearrange("(kc p) n -> p kc n", p=128)

    def norm_phase(srcs, wcol, AT3, eps=1e-6, NX=3):
        m0 = A.mark()
        xt = [A.f32(D) for _ in range(NX)]; junk = A.bf16(D)
        ss = [A.f32(1) for _ in range(NX)]; rs = [A.f32(1) for _ in range(NX)]
        Rxt = P.Rs(NX); Rss = P.Rs(NX); Rrs = P.Rs(NX); Rjk = P.R()
        tbr = ring(4); alt = ring(2)
        def stA(i, src):
            s = i % NX
            dma("sp", xt[s], src, "xt%d" % s, writes=[Rxt[s]])
            act(junk, xt[s], AF.Square, [Rxt[s]], [Rss[s], Rjk], accum_out=ss[s])
            rstd_chain(rs[s], ss[s], 1.0 / D, eps, Rss[s], Rrs[s])
            ts("dve", xt[s], xt[s], rs[s], None, ALU.mult, None, [Rxt[s], Rrs[s]], [Rxt[s]])

        gcnt = ring(8)

        def stB(i):
            s = i % NX
            for g in range(KC // 4):
                b = 4 + tbr()
                for q in range(4):
                    kc = g * 4 + q
                    tr(bank[b][:, q * 128:(q + 1) * 128], xt[s][:, kc * 128:(kc + 1) * 128], ident_f, [Rxt[s]], [Rbank[b]])
                on_act = gcnt() in (0, 3, 6)
                for q in range(4):
                    kc = g * 4 + q
                    o = AT3[:, kc, i * 128:(i + 1) * 128]; src_ = bank[b][:, q * 128:(q + 1) * 128]
                    if on_act:
                        act(o, src_, AF.Copy, [Rbank[b]], [], scale=wcol[:, kc:kc + 1])
                    else:
                        ts("dve", o, src_, wcol[:, kc:kc + 1], None, ALU.mult, None, [Rbank[b]], [])
        n_ = len(srcs)
        stA(0, srcs[0])
        for i in range(n_):
            if i + 1 < n_: stA(i + 1, srcs[i + 1])
            stB(i)
        P.barrier()
        A.release(m0)

    def load_AT(src, AT3, ntok):
        Rl = P.R()
        v = src.rearrange("(kc p) n -> p kc n", p=128)
        nk = v.shape[1]
        step = max(1, nk // 4)
        for k0 in range(0, nk, step):
            k1 = min(nk, k0 + step)
            dma("sp", AT3[:, k0:k1, 0:ntok], v[:, k0:k1, :], "lat", writes=[Rl])
        P.barrier()

    gb = ring(4)

    def gemm(jobs, AT3, KCn, CB, WSLOTS=2, jobs2=None, ratio=1):
        m0 = A.mark()
        wslot = [A.bf16(KCn * CB).rearrange("p (k c) -> p k c", c=CB) for _ in range(WSLOTS)]
        Rw = P.Rs(WSLOTS)
        wr = ring(WSLOTS)
        def mkitems(jobs_):
            its_ = []
            for jb in jobs_:
                for b in range(jb["ncols"] // CB):
                    c0 = jb["col0"] + b * CB
                    if jb["form"] == "F":
                        sub = [(m, t0, n) for m in range(CB // 128) for (t0, n) in jb["toks"]]
                    else:
                        sub = [(j,) for j in jb["tiles"]]
                    its_.append((jb, c0, sub))
            return its_
        items = mkitems(jobs)
        if jobs2:
            it2 = mkitems(jobs2); merged = []
            for it_ in items:
                merged.append(it_)
                for _ in range(ratio):
                    if it2: merged.append(it2.pop(0))
            items = merged + it2
        flat = [(jb, c0, su) for (jb, c0, sub) in items for su in sub]
        k = 0
        lastk = {}
        for k_, it_ in enumerate(flat): lastk[id(it_[0])] = k_
        if flat: flat[0][0]["pre"](0, flat[0])

        def loadW(bi):
            jb_, c0_, _ = items[bi]
            s_ = wr()
            dma("pool", wslot[s_], jb_["Wv"][:, :, c0_:c0_ + CB], "w%d" % s_, writes=[Rw[s_]])
            return s_
        wsl = {}
        for bi in range(min(WSLOTS - 1, len(items))): wsl[bi] = loadW(bi)
        for bi, (jb, c0, sub) in enumerate(items):
            nb_ = bi + WSLOTS - 1
            if nb_ < len(items): wsl[nb_] = loadW(nb_)
            s = wsl[bi]
            for su in sub:
                b = gb()
                ATj = jb.get("AT", AT3)
                if jb["form"] == "F":
                    m, t0, n = su
                    for kc in range(KCn):
                        mm(bank[b][:, 0:n], wslot[s][:, kc, m * 128:(m + 1) * 128], ATj[:, kc, t0:t0 + n],
                           kc == 0, kc == KCn - 1, [Rw[s]], [Rbank[b]])
                else:
                    (j,) = su
                    for kc in range(KCn):
                        mm(bank[b][:, 0:CB], ATj[:, kc, j * 128:(j + 1) * 128], wslot[s][:, kc, :],
                           kc == 0, kc == KCn - 1, [Rw[s]], [Rbank[b]])
                if k + 1 < len(flat): flat[k + 1][0]["pre"](k + 1, flat[k + 1])
                jb["epi"](k, (jb, c0, su), b)
                k += 1
                if (k == len(flat) or flat[k][0] is not jb) and "flush" in jb: jb["flush"]()
        P.barrier()
        A.release(m0)

    def nopre(k, item): pass

    def rope_bufs():
        return dict(cs=[A.f32(1024) for _ in range(3)], tb=[A.bf16(512) for _ in range(2)],
                    ta=[A.f32(512) for _ in range(2)], tb2=[A.f32(512) for _ in range(2)], ob=[A.bf16(512) for _ in range(2)],
                    Rcs=P.Rs(3), Rtb=P.Rs(2), Rta=P.Rs(2), Rtb2=P.Rs(2), Rob=P.Rs(2), rb=ring(2))

    def mk_epi_rope(B_, dest, dcol_of, ltok_of, colbase):
        cs, tb, ta, tb2, ob = B_["cs"], B_["tb"], B_["ta"], B_["tb2"], B_["ob"]
        Rcs, Rtb, Rta, Rtb2, Rob, rb = B_["Rcs"], B_["Rtb"], B_["Rta"], B_["Rtb2"], B_["Rob"], B_["rb"]

        def pre(k, item):
            jb, c0, (m, t0, n) = item
            s = k % 3; lt = ltok_of(t0)
            dma("sp", cs[s][:, 0:n], cosT[:, lt:lt + n], "cs%d" % s, writes=[Rcs[s]])
            dma("sp", cs[s][:, 512:512 + n], sinT[:, lt:lt + n], "cs%d" % s, writes=[Rcs[s]])

        pend = []

        def flush():
            while pend: pend.pop(0)()

        def epi(k, item, b):
            jb, c0, (m, t0, n) = item
            s = k % 2; s3 = k % 3; hb = (c0 - colbase) // 128 + m
            act(tb[s][:, 0:n], bank[b][:, 0:n], AF.Copy, [Rbank[b]], [Rtb[s]])
            tt("dve", ta[s][:, 0:n], bank[b][:, 0:n], cs[s3][:, 0:n], ALU.mult, [Rbank[b], Rcs[s3]], [Rta[s]])
            flush()

            def phase2():
                r = 6 + rb()
                mm(bank[r][:, 0:n], Rm_b, tb[s][:, 0:n], True, True, [Rtb[s]], [Rbank[r]])
                tt("dve", tb2[s][:, 0:n], bank[r][:, 0:n], cs[s3][:, 512:512 + n], ALU.mult, [Rbank[r], Rcs[s3]], [Rtb2[s]])
                tt("dve", ob[s][:, 0:n], ta[s][:, 0:n], tb2[s][:, 0:n], ALU.add, [Rta[s], Rtb2[s]], [Rob[s]])
                dc = dcol_of(t0)
                dma("sp", dest[hb * 128:(hb + 1) * 128, dc:dc + n], ob[s][:, 0:n], "ob%d" % s, reads=[Rob[s]])
            pend.append(phase2)
        epi.flush = flush
        return pre, epi

    def mk_epi_plainF(dest, dcol_of, colbase, nm):
        ob = [A.bf16(512) for _ in range(2)]; Rob = P.Rs(2); alt = ring(2)

        def epi(k, item, b):
            jb, c0, (m, t0, n) = item
            s = k % 2; hb = (c0 - colbase) // 128 + m
            cpy("act" if alt() == 0 else "dve", ob[s][:, 0:n], bank[b][:, 0:n], [Rbank[b]], [Rob[s]])
            dc = dcol_of(t0)
            dma("sp", dest[hb * 128:(hb + 1) * 128, dc:dc + n], ob[s][:, 0:n], nm + "%d" % s, reads=[Rob[s]])
        return nopre, epi

    def mk_epi_V(dest, drow_of, colbase, CB, use_flag, nm):
        ob = [A.bf16(CB) for _ in range(2)]; Rob = P.Rs(2)

        def epi(k, item, b):
            jb, c0, (j,) = item
            s = k % 2
            if use_flag:
                act(ob[s], bank[b][:, 0:CB], AF.Copy, [Rbank[b]], [Rob[s]], scale=flag[:, 0:1])
            else:
                cpy("act", ob[s], bank[b][:, 0:CB], [Rbank[b]], [Rob[s]])
            r0 = drow_of(j)
            dma("sp", dest[r0:r0 + 128, c0 - colbase:c0 - colbase + CB], ob[s], nm + "%d" % s, reads=[Rob[s]])
        return nopre, epi

    def mk_epi_resid(src, srow_of, dest, drow_of, CB):
        xs = [A.f32(CB) for _ in range(2)]; ob = [A.f32(CB) for _ in range(2)]
        Rxs = P.Rs(2); Rob = P.Rs(2)

        def pre(k, item):
            jb, c0, (j,) = item
            s = k % 2; r0 = srow_of(j)
            dma("sp", xs[s], src[r0:r0 + 128, c0:c0 + CB], "xs%d" % s, writes=[Rxs[s]])

        def epi(k, item, b):
            jb, c0, (j,) = item
            s = k % 2
            tt("dve", ob[s], bank[b][:, 0:CB], xs[s], ALU.add, [Rbank[b], Rxs[s]], [Rob[s]])
            r0 = drow_of(j)
            dma("sp", dest[r0:r0 + 128, c0:c0 + CB], ob[s], "ro%d" % s, reads=[Rob[s]])
        return pre, epi

    def chunks(t0, t1):
        r = []
        while t0 < t1:
            n = min(512, t1 - t0); r.append((t0, n)); t0 += n
        return r

    CBW = 256
    Win = wv(w_in)
    WA, WB = c.WA, c.WB
    cq_a, ck_a, cv_a = 0, WA, 2 * WA
    cq_b, ck_b, cv_b = 3 * WA, 3 * WA + WB, 3 * WA + 2 * WB
    def alloc_AT(ntok):
        return A.bf16(KC * ntok).rearrange("p (k n) -> p k n", n=ntok)

    if upto < 1:
        return finish()
    TP = c.TP
    AT3 = alloc_AT(TP * 128)
    import os
    if os.environ.get("K_NONORM") != "1":
      norm_phase([xl[i * 128:(i + 1) * 128, :] for i in range(TP)], ncols[:, 0:KC], AT3)
    pre_r, epi_r = mk_epi_rope(rope_bufs(), KTa, lambda t0: t0, lambda t0: t0, ck_a)
    _, epi_va = mk_epi_V(Va, lambda j: j * 128, cv_a, CBW, True, "va")
    _, epi_kb = mk_epi_plainF(KTb, lambda t0: t0, ck_b, "kb")
    _, epi_vb = mk_epi_V(Vb, lambda j: j * 128, cv_b, CBW, True, "vb")
    JSEL = os.environ.get("K_JOBS", "0123")
    if upto == 1:
        gemm([jj for ii, jj in enumerate([
        dict(form="F", Wv=Win, col0=ck_a, ncols=WA, toks=chunks(0, TP * 128), pre=pre_r, epi=epi_r, flush=epi_r.flush),
        dict(form="T", Wv=Win, col0=cv_a, ncols=WA, tiles=list(range(TP)), pre=nopre, epi=epi_va),
        dict(form="F", Wv=Win, col0=ck_b, ncols=WB, toks=chunks(c.KB0 * 128, TP * 128), pre=nopre, epi=epi_kb),
        dict(form="T", Wv=Win, col0=cv_b, ncols=WB, tiles=list(range(c.KB0, TP)), pre=nopre, epi=epi_vb)]) if str(ii) in JSEL], AT3, KC, CBW)
        return finish()
    gemm([
        dict(form="F", Wv=Win, col0=ck_a, ncols=WA, toks=chunks(0, TP * 128), pre=pre_r, epi=epi_r, flush=epi_r.flush),
        dict(form="T", Wv=Win, col0=cv_a, ncols=WA, tiles=list(range(TP)), pre=nopre, epi=epi_va),
        dict(form="F", Wv=Win, col0=ck_b, ncols=WB, toks=chunks(c.KB0 * 128, TP * 128), pre=nopre, epi=epi_kb),
        dict(form="T", Wv=Win, col0=cv_b, ncols=WB, tiles=list(range(c.KB0, TP)), pre=nopre, epi=epi_vb),
    ], AT3, KC, CBW)
    A.release(PM)

    if upto < 2:
        return finish()
    QT0 = c.QT0
    qbase = QT0 * 128
    TH = c.TO // 2
    MEM = c.MEM
    for (qa0, qa1) in ((0, TH + 1), (TH + 1, c.NQT)):
        nt = qa1 - qa0
        tb0 = qa0 * 128
        jobs2 = None
        if qa0 != 0:
            ATm = alloc_AT(MEM)
            norm_phase([memb[i * 128:(i + 1) * 128, :] for i in range(MEM // 128)], ncols[:, 2 * KC:3 * KC], ATm)
            _, epi_mk = mk_epi_plainF(KmT, lambda t0: t0, 0, "mk")
            _, epi_mv = mk_epi_V(Vm, lambda j: j * 128, D, CBW, False, "mv")
            jobs2 = [dict(form="F", Wv=wv(w_mkv), col0=0, ncols=D, toks=chunks(0, MEM), pre=nopre, epi=epi_mk, AT=ATm),
                     dict(form="T", Wv=wv(w_mkv), col0=D, ncols=D, tiles=list(range(MEM // 128)), pre=nopre, epi=epi_mv, AT=ATm)]
        AT3 = alloc_AT(nt * 128)
        norm_phase([xl[(QT0 + qa0 + i) * 128:(QT0 + qa0 + i + 1) * 128, :] for i in range(nt)], ncols[:, 0:KC], AT3)
        RB = rope_bufs()
        pre_q, epi_q = mk_epi_rope(RB, QTa, lambda t0, tb0=tb0: tb0 + t0, lambda t0, tb0=tb0: qbase + tb0 + t0, cq_a)
        pre_k, epi_k = mk_epi_rope(RB, KTa, lambda t0, tb0=tb0: qbase + tb0 + t0, lambda t0, tb0=tb0: qbase + tb0 + t0, ck_a)
        _, epi_va = mk_epi_V(Va, lambda j, qa0=qa0: (QT0 + qa0 + j) * 128, cv_a, CBW, False, "va")
        _, epi_qb = mk_epi_plainF(QTb, lambda t0, tb0=tb0: tb0 + t0, cq_b, "qb")
        _, epi_kb = mk_epi_plainF(KTb, lambda t0, tb0=tb0: qbase + tb0 + t0, ck_b, "kb")
        _, epi_vb = mk_epi_V(Vb, lambda j, qa0=qa0: (QT0 + qa0 + j) * 128, cv_b, CBW, False, "vb")
        k0t = 1 if qa0 == 0 else 0
        own = list(range(k0t, nt))
        gemm([
            dict(form="F", Wv=Win, col0=cq_a, ncols=WA, toks=chunks(0, nt * 128), pre=pre_q, epi=epi_q, flush=epi_q.flush),
            dict(form="F", Wv=Win, col0=ck_a, ncols=WA, toks=chunks(k0t * 128, nt * 128), pre=pre_k, epi=epi_k, flush=epi_k.flush),
            dict(form="T", Wv=Win, col0=cv_a, ncols=WA, tiles=own, pre=nopre, epi=epi_va),
            dict(form="F", Wv=Win, col0=cq_b, ncols=WB, toks=chunks(0, nt * 128), pre=nopre, epi=epi_qb),
            dict(form="F", Wv=Win, col0=ck_b, ncols=WB, toks=chunks(k0t * 128, nt * 128), pre=nopre, epi=epi_kb),
            dict(form="T", Wv=Win, col0=cv_b, ncols=WB, tiles=own, pre=nopre, epi=epi_vb),
        ], AT3, KC, CBW, WSLOTS=4, jobs2=jobs2, ratio=1)
        A.release(PM)

    if upto < 3:
        return finish()
    NLT, NQT = c.NLT, c.NQT
    sc = 128 ** -0.5
    TINY = 1e-30

    def attn_A():
        m0 = A.mark()
        KT = [A.bf16(2 * L).rearrange("p (m l) -> p m l", m=2) for _ in range(2)]
        QT = [A.bf16(2 * NQ).rearrange("p (m l) -> p m l", m=2) for _ in range(2)]
        V = [A.bf16(NLT * 258).rearrange("p (t d) -> p t d", d=258) for _ in range(2)]
        oT = [A.bf16(2 * NQ).rearrange("p (m l) -> p m l", m=2) for _ in range(2)]
        ET = [A.bf16(512) for _ in range(4)]
        t0b = [A.f32(256) for _ in range(4)]; ob = [A.f32(256) for _ in range(4)]; onb = [A.f32(256) for _ in range(4)]
        sw = A.f32(256); junk = A.bf16(256)
        rr = [A.f32(2) for _ in range(4)]; c1 = [A.f32(1) for _ in range(4)]
        ss = [A.f32(1) for _ in range(4)]; rs = [A.f32(1) for _ in range(4)]
        RKT = P.Rs(2); RQT = P.Rs(2); RV = P.Rs(2); RoT = P.Rs(2); RET = P.Rs(4)
        Rt0 = P.Rs(4); Ro = P.Rs(4); Ron = P.Rs(4); Rrr = P.Rs(4); Rc1 = P.Rs(4); Rss = P.Rs(4); Rrs = P.Rs(4)
        Rsw = P.R(); Rjk = P.R()
        lnpost = A.f32(1)
        memset("dve", lnpost, math.log(1.0 - lam_init), [], [Rsw])
        dma("sp", sw, subln_d, "c0", writes=[Rsw])
        for s in range(2):
            memset("pool", V[s][:, :, 256:258], 1.0, [], [RV[s]])
            ts("dve", V[s][:, 0:TP, 256:257], V[s][:, 0:TP, 256:257], flag[:, 0:1], None, ALU.mult, None, [RV[s]], [RV[s]])
        sbr = ring(3); er = ring(4); tbr = ring(1)
        Vav = Va.rearrange("(t p) c -> p t c", p=128)

        def load_head(h):
            s = h % 2
            dma("sp", KT[s], KTa[2 * h * 128:(2 * h + 2) * 128, :].rearrange("(m p) l -> p m l", p=128), "ak%d" % s, writes=[RKT[s]])
            dma("sp", QT[s], QTa[2 * h * 128:(2 * h + 2) * 128, :].rearrange("(m p) l -> p m l", p=128), "aq%d" % s, writes=[RQT[s]])
            dma("sp", V[s][:, :, 0:256], Vav[:, :, h * 256:(h + 1) * 256], "av%d" % s, writes=[RV[s]])

        groups = []
        for h in range(c.NDH):
            for i in range(NQT):
                nk = QT0 + i + 1
                for g in range((nk + 1) // 2):
                    groups.append(dict(h=h, i=i, nk=nk, kts=[kt for kt in (2 * g, 2 * g + 1) if kt < nk],
                                       last=(g == (nk + 1) // 2 - 1)))

        def qk(G):
            s = G["h"] % 2; i = G["i"]
            sb = sbr(); G["sb"] = sb
            for j, kt in enumerate(G["kts"]):
                for m in range(2):
                    mm(bank[sb][:, (j * 2 + m) * 128:(j * 2 + m + 1) * 128], KT[s][:, m, kt * 128:(kt + 1) * 128],
                       QT[s][:, m, i * 128:(i + 1) * 128], True, True, [RKT[s], RQT[s]], [Rbank[sb]])

        def rest(G):
            h = G["h"]; s = h % 2; i = G["i"]; nk = G["nk"]; kts = G["kts"]; sb = G["sb"]; T = nk - 1
            Ob = [3 + (i % 2) * 2, 4 + (i % 2) * 2]
            ncol = len(kts) * 256
            e = er()
            act(ET[e][:, 0:ncol], bank[sb][:, 0:ncol], AF.Exp, [Rbank[sb]], [RET[e]], scale=sc)
            if T in kts:
                j = kts.index(T)
                dz = ET[e][64:128, j * 256:(j + 1) * 256].rearrange("p (m q) -> p m q", m=2)[:, :, 0:64]
                act(dz, dz, AF.Copy, [RET[e]], [RET[e]], scale=0.0)
            for j, kt in enumerate(kts):
                for m in range(2):
                    mm(bank[Ob[m]][:, 0:257], ET[e][:, (j * 2 + m) * 128:(j * 2 + m + 1) * 128], V[s][:, kt, 0:257],
                       kt == 0, kt == nk - 1, [RET[e], RV[s]], [Rbank[Ob[m]]])
            if not G["last"]: return
            G["fin"] = True

        def F1(G):
            i = G["i"]; f = i % 4
            Ob = [3 + (i % 2) * 2, 4 + (i % 2) * 2]
            for m in range(2):
                ts("dve", rr[f][:, m:m + 1], bank[Ob[m]][:, 256:257], TINY, None, ALU.add, None, [Rbank[Ob[m]]], [Rrr[f]])
            recip(rr[f], rr[f], [Rrr[f]], [Rrr[f]])
            tt("dve", c1[f], rr[f][:, 1:2], neglam, ALU.mult, [Rrr[f]], [Rc1[f]])
            ts("dve", t0b[f], bank[Ob[0]][:, 0:256], rr[f][:, 0:1], None, ALU.mult, None, [Rbank[Ob[0]], Rrr[f]], [Rt0[f]])
            stt(ob[f], bank[Ob[1]][:, 0:256], c1[f][:, 0:1], t0b[f], ALU.mult, ALU.add, [Rbank[Ob[1]], Rc1[f], Rt0[f]], [Ro[f]])
            P.emit("dve", lambda e: e.scalar_tensor_tensor(out=junk, in0=ob[f], scalar=1.0, in1=ob[f], op0=ALU.mult, op1=ALU.mult,
                                                           accum_out=ss[f]), [Ro[f]], [Rss[f], Rjk])
            ts("dve", rs[f], ss[f], 1.0 / 256, 1e-6, ALU.mult, ALU.add, [Rss[f]], [Rrs[f]])

        def F2(G):
            i = G["i"]; f = i % 4
            act(rs[f], rs[f], AF.Ln, [Rrs[f]], [Rrs[f]])
            act(rs[f], rs[f], AF.Exp, [Rrs[f]], [Rrs[f]], scale=-0.5, bias=lnpost)
            stt(onb[f], ob[f], rs[f][:, 0:1], sw, ALU.mult, ALU.mult, [Ro[f], Rrs[f], Rsw], [Ron[f]])

        def fin_tr(G):
            h = G["h"]; s = h % 2; i = G["i"]; f = i % 4
            tb_ = 7 + tbr()
            for cc in range(2):
                tr(bank[tb_][:, cc * 128:(cc + 1) * 128], onb[f][:, cc * 128:(cc + 1) * 128], ident_f, [Ron[f]], [Rbank[tb_]])
            cpy("dve", oT[s][:, :, i * 128:(i + 1) * 128], bank[tb_][:, 0:256].rearrange("p (c q) -> p c q", c=2),
                [Rbank[tb_]], [RoT[s]])
            if i == NQT - 1:
                dma("sp", ATmix[h * 256:(h + 1) * 256, :].rearrange("(c p) n -> p c n", p=128), oT[s], "ao%d" % s, reads=[RoT[s]])

        load_head(0)
        if c.NDH > 1: load_head(1)
        P.bg.add("wc")
        rstep = 256 if c.DFF % 256 == 0 else 128
        casts = list(range(0, c.DFF, rstep))
        pace = [A.f32(1) for _ in range(4)]; Rpace = P.Rs(4)
        cst_i = [0, 0]

        def cast_step(paced):
            if not casts: return
            r0 = casts.pop(0)
            k_ = cst_i[0] % 4; cst_i[0] += 1
            rd = []
            if paced:
                memset("dve", pace[k_], 0.0, [], [Rpace[k_]])
                rd = [Rpace[k_]]
            dma("pool", WDb[r0:r0 + rstep, :], w_down[r0:r0 + rstep, :], "wc", reads=rd)
        LA = 2
        for n_ in range(min(LA, len(groups))): qk(groups[n_])
        sched = []

        def run_due(n_, force=False):
            while sched and (force or sched[0][0] <= n_):
                _, fn_, G_ = sched.pop(0)
                fn_(G_)
        for n_, G in enumerate(groups):
            if n_ + LA < len(groups):
                qk(groups[n_ + LA])
            rest(G)
            if G.get("fin"):
                F1(G)
                cst_i[1] += 1
                if cst_i[1] % 3 == 0: cast_step(True)
                sched.append((n_ + 6, F2, G))
                sched.append((n_ + 9, fin_tr, G))
                sched.sort(key=lambda t: t[0])
            run_due(n_)
            if G["last"] and G["i"] == NQT - 1:
                run_due(n_, force=True)
                if G["h"] + 2 < c.NDH: load_head(G["h"] + 2)
        while casts: cast_step(False)

        P.barrier()
        A.release(m0)

    attn_A()

    if upto < 4:
        return finish()
    def attn_B():
        m0 = A.mark()
        KB0 = c.KB0; NKT = NLT - KB0
        KT = [A.bf16(NKT * 128) for _ in range(2)]
        QT = [A.bf16(NQ) for _ in range(2)]
        V = [A.bf16(NKT * 130).rearrange("p (t d) -> p t d", d=130) for _ in range(2)]
        BT = [A.f32(5 * 128) for _ in range(2)]
        oT = [A.bf16(NQ) for _ in range(2)]
        SB = [A.f32(640) for _ in range(3)]
        ET = [A.bf16(640) for _ in range(4)]
        ob = [A.f32(128) for _ in range(4)]; rr = [A.f32(1) for _ in range(4)]
        RKT = P.Rs(2); RQT = P.Rs(2); RV = P.Rs(2); RBT = P.Rs(2); RoT = P.Rs(2); RSB = P.Rs(3); RET = P.Rs(4)
        Ro = P.Rs(4); Rrr = P.Rs(4)
        npv = TP - KB0
        for s in range(2):
            memset("pool", V[s][:, :, 128:130], 1.0, [], [RV[s]])
            ts("dve", V[s][:, 0:npv, 128:129], V[s][:, 0:npv, 128:129], flag[:, 0:1], None, ALU.mult, None, [RV[s]], [RV[s]])
        sbr = ring(4); er = ring(4); tbk = ring(4); fr = ring(4); sr = ring(3)
        Vbv = Vb.rearrange("(t p) c -> p t c", p=128)

        def load_head(h):
            s = h % 2
            dma("sp", KT[s], KTb[h * 128:(h + 1) * 128, KB0 * 128:L], "bk%d" % s, writes=[RKT[s]])
            dma("sp", QT[s], QTb[h * 128:(h + 1) * 128, :], "bq%d" % s, writes=[RQT[s]])
            dma("sp", V[s][:, :, 0:128], Vbv[:, KB0:NLT, h * 128:(h + 1) * 128], "bv%d" % s, writes=[RV[s]])
            dma("sp", BT[s], bt_d[:, h * 640:(h + 1) * 640], "bb%d" % s, writes=[RBT[s]])
            memset("pool", BT[s][64:128, 512:576], -30000.0, [], [RBT[s]])
            memset("pool", BT[s][0:64, 64:128], -30000.0, [], [RBT[s]])

        groups = [dict(h=h, i=i) for h in range(c.NBH) for i in range(NQT)]

        def qk(G):
            s = G["h"] % 2; i = G["i"]; T = QT0 + i
            sb = sbr(); G["sb"] = sb
            tbk_ = 4 + tbk(); G["tb"] = tbk_
            for dd in range(5):
                kt = T - 4 + dd - KB0
                dst = bank[sb][:, dd * 128:(dd + 1) * 128] if dd < 4 else bank[tbk_][:, 0:128]
                mm(dst, KT[s][:, kt * 128:(kt + 1) * 128], QT[s][:, i * 128:(i + 1) * 128], True, True,
                   [RKT[s], RQT[s]], [Rbank[sb] if dd < 4 else Rbank[tbk_]])

        def st1(G):
            h = G["h"]; s = h % 2; sb = G["sb"]; tbk_ = G["tb"]
            q = sr()
            stt(SB[q][:, 0:512], bank[sb][:, 0:512], sc, BT[s][:, 0:512], ALU.mult, ALU.add, [Rbank[sb], RBT[s]], [RSB[q]])
            stt(SB[q][:, 512:640], bank[tbk_][:, 0:128], sc, BT[s][:, 512:640], ALU.mult, ALU.add, [Rbank[tbk_], RBT[s]], [RSB[q]])
            e = er(); G["e"] = e
            act(ET[e], SB[q], AF.Exp, [RSB[q]], [RET[e]])

        def st2(G):
            h = G["h"]; s = h % 2; i = G["i"]; T = QT0 + i; o_ = G["tb"]; e = G["e"]
            for dd in range(5):
                kt = T - 4 + dd - KB0
                mm(bank[o_][:, 128:257], ET[e][:, dd * 128:(dd + 1) * 128], V[s][:, kt, 0:129], dd == 0, dd == 4,
                   [RET[e], RV[s]], [Rbank[o_]])

        def st3(G):
            o_ = G["tb"]
            f = fr(); G["f"] = f
            ts("dve", rr[f], bank[o_][:, 256:257], TINY, None, ALU.add, None, [Rbank[o_]], [Rrr[f]])
            recip(rr[f], rr[f], [Rrr[f]], [Rrr[f]])
            act(ob[f], bank[o_][:, 128:256], AF.Copy, [Rbank[o_], Rrr[f]], [Ro[f]], scale=rr[f][:, 0:1])

        def fin_tr(G):
            h = G["h"]; s = h % 2; i = G["i"]; f = G["f"]
            tb_ = G["tb"]
            tr(bank[tb_][:, 384:512], ob[f], ident_f, [Ro[f]], [Rbank[tb_]])
            cpy("act", oT[s][:, i * 128:(i + 1) * 128], bank[tb_][:, 384:512], [Rbank[tb_]], [RoT[s]])
            if i == NQT - 1:
                dma("sp", ATmix[WA + h * 128:WA + (h + 1) * 128, :], oT[s], "bo%d" % s, reads=[RoT[s]])

        load_head(0)
        if c.NBH > 1: load_head(1)
        LA = 2
        NG_ = len(groups)
        for n_ in range(min(LA, NG_)): qk(groups[n_])
        st1(groups[0])
        for n_, G in enumerate(groups):
            if n_ + LA < NG_: qk(groups[n_ + LA])
            if n_ + 1 < NG_: st1(groups[n_ + 1])
            st2(G)
            if n_ >= 1: st3(groups[n_ - 1])
            if n_ >= 2: fin_tr(groups[n_ - 2])
            if G["i"] == 2 and G["h"] >= 1 and G["h"] + 1 < c.NBH:
                load_head(G["h"] + 1)
        st3(groups[-1])
        if NG_ >= 2: fin_tr(groups[-2])
        fin_tr(groups[-1])

        P.barrier()
        A.release(m0)

    attn_B()

    if upto < 7:
        return finish()
    AT3 = alloc_AT(NQ)
    load_AT(ATmix, AT3, NQ)
    pre_x, epi_x = mk_epi_resid(xl, lambda j: (QT0 + j) * 128, X1, lambda j: j * 128, CBW)
    gemm([dict(form="T", Wv=wv(w_out), col0=0, ncols=D, tiles=list(range(NQT)), pre=pre_x, epi=epi_x)], AT3, KC, CBW, WSLOTS=3)
    A.release(PM)
    AT3 = alloc_AT(NQ)
    norm_phase([X1[i * 128:(i + 1) * 128, :] for i in range(NQT)], ncols[:, KC:2 * KC], AT3)
    _, epi_mq = mk_epi_plainF(QmT, lambda t0: t0, 0, "mq")
    gemm([dict(form="F", Wv=wv(w_mq), col0=0, ncols=D, toks=chunks(0, NQ), pre=nopre, epi=epi_mq)], AT3, KC, CBW, WSLOTS=3)
    A.release(PM)

    if upto < 8:
        return finish()
    def attn_M():
        m0 = A.mark()
        MC = c.MHD // 128
        NMT = MEM // 128
        Km = A.bf16(KC * MEM).rearrange("p (k n) -> p k n", n=MEM)
        Vv = A.bf16(NMT * D).rearrange("p (t d) -> p t d", d=D)
        Qc = [A.bf16(MC * 512).rearrange("p (k n) -> p k n", n=512) for _ in range(3)]
        ET = [A.bf16(NMT * 512).rearrange("p (t n) -> p t n", n=512) for _ in range(3)]
        Rl = [A.f32(512) for _ in range(2)]
        oS = [A.bf16(512) for _ in range(3)]
        RK = P.R(); RQ = P.Rs(3); RET = P.Rs(3); RRl = P.Rs(2); RoS = P.Rs(3)
        dma("sp", Km, KmT.rearrange("(k p) n -> p k n", p=128), "mk0", writes=[RK])
        dma("sp", Vv, Vm.rearrange("(t p) d -> p t d", p=128), "mk0", writes=[RK])
        scm = c.MHD ** -0.5
        qr = ring(3); sbr = ring(2); er = ring(3); obr = ring(3); osr = ring(3); lr = ring(2)
        QmTv = QmT.rearrange("(k p) n -> p k n", p=128)
        ATmov = ATmo.rearrange("(k p) n -> p k n", p=128)
        alt = ring(2)
        its = [dict(h=h, t0=t0, n=n) for h in range(c.NMH) for (t0, n) in chunks(0, NQ)]

        def loadQ(I):
            h, t0, n = I["h"], I["t0"], I["n"]
            qs = qr(); I["qs"] = qs
            dma("sp", Qc[qs][:, :, 0:n], QmTv[:, h * MC:(h + 1) * MC, t0:t0 + n], "mq%d" % qs, writes=[RQ[qs]])

        def part1(I):
            h, t0, n, qs = I["h"], I["t0"], I["n"], I["qs"]
            e = er(); I["e"] = e
            for mt in range(NMT):
                sb = sbr()
                for k in range(MC):
                    mm(bank[sb][:, 0:n], Km[:, h * MC + k, mt * 128:(mt + 1) * 128], Qc[qs][:, k, 0:n], k == 0, k == MC - 1,
                       [RK, RQ[qs]], [Rbank[sb]])
                act(ET[e][:, mt, 0:n], bank[sb][:, 0:n], AF.Exp, [Rbank[sb]], [RET[e]], scale=scm)

        def part2(I):
            h, t0, n, e = I["h"], I["t0"], I["n"], I["e"]
            lb = 2
            for mt in range(NMT):
                mm(bank[lb][:, 0:n], ones_b, ET[e][:, mt, 0:n], mt == 0, mt == NMT - 1, [RET[e]], [Rbank[lb]])
            l_ = lr()
            act(Rl[l_][:, 0:n], bank[lb][:, 0:n], AF.Ln, [Rbank[lb]], [RRl[l_]])
            act(Rl[l_][:, 0:n], Rl[l_][:, 0:n], AF.Exp, [RRl[l_]], [RRl[l_]], scale=-1.0)
            for k in range(MC):
                o_ = 3 + obr()
                for mt in range(NMT):
                    mm(bank[o_][:, 0:n], Vv[:, mt, (h * MC + k) * 128:(h * MC + k + 1) * 128], ET[e][:, mt, 0:n],
                       mt == 0, mt == NMT - 1, [RK, RET[e]], [Rbank[o_]])
                q = osr()
                tt("dve", oS[q][:, 0:n], bank[o_][:, 0:n], Rl[l_][:, 0:n], ALU.mult, [Rbank[o_], RRl[l_]], [RoS[q]])
                dma("sp", ATmov[:, h * MC + k, t0:t0 + n], oS[q][:, 0:n], "mo%d" % q, reads=[RoS[q]])

        for n_ in range(min(2, len(its))): loadQ(its[n_])
        part1(its[0])
        for n_, I in enumerate(its):
            if n_ + 2 < len(its): loadQ(its[n_ + 2])
            if n_ + 1 < len(its): part1(its[n_ + 1])
            part2(I)
        P.barrier()
        A.release(m0)

    attn_M()

    if upto < 9:
        return finish()
    AT3 = alloc_AT(NQ)
    load_AT(ATmo, AT3, NQ)
    pre_x, epi_x = mk_epi_resid(X1, lambda j: j * 128, X2, lambda j: j * 128, CBW)
    gemm([dict(form="T", Wv=wv(w_mo), col0=0, ncols=D, tiles=list(range(NQT)), pre=pre_x, epi=epi_x)], AT3, KC, CBW, WSLOTS=3)
    A.release(PM)

    if upto < 10:
        return finish()
    def ffn_up(hf):
        NT = TH + 1; NQh = NT * 128; NOh = TH * 128
        AT3 = alloc_AT(NQh)
        norm_phase([X2[(hf * TH + i) * 128:(hf * TH + i + 1) * 128, :] for i in range(NT)], ncols[:, 3 * KC:4 * KC], AT3)
        m0 = A.mark()
        DFF, NFB = c.DFF, c.NFB
        Wup = wv(w_up)
        wg = [A.bf16(KC * 128).rearrange("p (k c) -> p k c", c=128) for _ in range(2)]
        wu = [A.bf16(KC * 128).rearrange("p (k c) -> p k c", c=128) for _ in range(2)]
        G = [A.f32(NQh) for _ in range(2)]
        Cv = [A.f32(NOh) for _ in range(2)]
        Sg = [A.f32(NOh) for _ in range(2)]
        aT = [A.bf16(NOh) for _ in range(2)]
        cw = A.f32(NFB * 4)
        RW = P.Rs(2); RG = P.Rs(2); RC = P.Rs(2); RS = P.Rs(2); RaT = P.Rs(2); Rcw = P.R()
        dma("sp", cw, cw_d, "c0", writes=[Rcw])
        gch = [(126, 2)] + chunks(128, NQh); uch = chunks(128, NQh)
        for j in range(NFB):
            s = j % 2
            dma("pool", wg[s], Wup[:, :, j * 128:(j + 1) * 128], "ug%d" % s, writes=[RW[s]])
            dma("pool", wu[s], Wup[:, :, DFF + j * 128:DFF + (j + 1) * 128], "ug%d" % s, writes=[RW[s]])
            for (t0, n) in gch:
                b = gb()
                for kc in range(KC):
                    mm(bank[b][:, 0:n], wg[s][:, kc, :], AT3[:, kc, t0:t0 + n], kc == 0, kc == KC - 1, [RW[s]], [Rbank[b]])
                if t0 < 128:
                    act(G[s][:, t0:t0 + n], bank[b][:, 0:n], AF.Copy, [Rbank[b]], [RG[s]],
                        scale=(flag[:, 0:1] if hf == 0 else 1.0))
                else:
                    cpy("act", G[s][:, t0:t0 + n], bank[b][:, 0:n], [Rbank[b]], [RG[s]])
            w0 = cw[:, 4 * j:4 * j + 1]; w1 = cw[:, 4 * j + 1:4 * j + 2]; w2 = cw[:, 4 * j + 2:4 * j + 3]; cb = cw[:, 4 * j + 3:4 * j + 4]
            ts("dve", Cv[s], G[s][:, 128:NQh], w2, cb, ALU.mult, ALU.add, [RG[s], Rcw], [RC[s]])
            stt(Cv[s], G[s][:, 127:NQh - 1], w1, Cv[s], ALU.mult, ALU.add, [RG[s], RC[s]], [RC[s]])
            stt(Cv[s], G[s][:, 126:NQh - 2], w0, Cv[s], ALU.mult, ALU.add, [RG[s], RC[s]], [RC[s]])
            act(Sg[s], Cv[s], AF.Silu, [RC[s]], [RS[s]])
            for (t0, n) in uch:
                b = gb()
                for kc in range(KC):
                    mm(bank[b][:, 0:n], wu[s][:, kc, :], AT3[:, kc, t0:t0 + n], kc == 0, kc == KC - 1, [RW[s]], [Rbank[b]])
                tt("dve", aT[s][:, t0 - 128:t0 - 128 + n], bank[b][:, 0:n], Sg[s][:, t0 - 128:t0 - 128 + n], ALU.mult,
                   [Rbank[b], RS[s]], [RaT[s]])
            dma("sp", ACTT[j * 128:(j + 1) * 128, hf * NOh:(hf + 1) * NOh], aT[s], "ua%d" % s, reads=[RaT[s]])
        P.barrier()
        A.release(PM)

    ffn_up(0)
    ffn_up(1)

    if upto < 11:
        return finish()
    def ffn_down():
        m0 = A.mark()
        NFB = c.NFB
        GT = 4 if c.TO % 4 == 0 else 1
        NG = c.TO // GT
        KG = 4 if NFB >= 4 else NFB
        CB = 512 if D >= 512 else D
        NCB = D // CB
        Wd = wv(WDb)
        P.join_bg("pool")
        Ag = A.bf16(NFB * GT * 128).rearrange("p (k n) -> p k n", n=GT * 128)
        X3 = [A.f32(D) for _ in range(GT)]
        wp = [A.bf16(KG * CB).rearrange("p (k c) -> p k c", c=CB) for _ in range(3)]
        xs = [A.f32(CB) for _ in range(2)]
        fw = A.f32(D)
        junk = A.bf16(CB)
        ssq = [A.f32(NCB) for _ in range(GT)]; RSQ = P.Rs(GT); Rjk = P.R()
        ss = [A.f32(1) for _ in range(2)]; rs = [A.f32(1) for _ in range(2)]
        RA = P.Rs(4); RX3 = P.Rs(GT); Rwp = P.Rs(3); Rxs = P.Rs(2); Rfw = P.R(); Rss = P.Rs(2); Rrs = P.Rs(2)
        dma("sp", fw, fnw_d, "c0", writes=[Rfw])
        ACTv = ACTT.rearrange("(k p) n -> p k n", p=128)
        wpr = ring(3); xr = ring(2); fr = ring(2)
        pieces = [(k0, min(NFB, k0 + KG)) for k0 in range(0, NFB, KG)]
        qstep = (NFB + 3) // 4
        qof = lambda k: min(3, k // qstep)

        def load_Ag(g):
            for qi in range(4):
                k0 = qi * qstep; k1 = min(NFB, k0 + qstep)
                if k0 < k1:
                    dma("sp", Ag[:, k0:k1, :], ACTv[:, k0:k1, g * GT * 128:(g + 1) * GT * 128], "da%d" % qi, writes=[RA[qi]])
        load_Ag(0)
        for g in range(NG):
            for cb_ in range(NCB):
                c0 = cb_ * CB
                bs = [(cb_ % 2) * 4 + t for t in range(GT)]
                for (k0, k1) in pieces:
                    w_ = wpr()
                    dma("pool", wp[w_][:, 0:k1 - k0, :], Wd[:, k0:k1, c0:c0 + CB], "dw%d" % w_, writes=[Rwp[w_]])
                    for t in range(GT):
                        for k in range(k0, k1):
                            mm(bank[bs[t]][:, 0:CB], Ag[:, k, t * 128:(t + 1) * 128], wp[w_][:, k - k0, :], k == 0, k == NFB - 1,
                               [RA[qof(k)], Rwp[w_]], [Rbank[bs[t]]])
                for t in range(GT):
                    x_ = xr()
                    r0 = 128 + (g * GT + t) * 128
                    dma("sp", xs[x_], X2[r0:r0 + 128, c0:c0 + CB], "dx%d" % x_, writes=[Rxs[x_]])
                    tt("dve", X3[t][:, c0:c0 + CB], bank[bs[t]][:, 0:CB], xs[x_], ALU.add, [Rbank[bs[t]], Rxs[x_]], [RX3[t]])
                    act(junk, X3[t][:, c0:c0 + CB], AF.Square, [RX3[t]], [RSQ[t], Rjk], accum_out=ssq[t][:, cb_:cb_ + 1])
            if g + 1 < NG: load_Ag(g + 1)
            for t in range(GT):
                f = fr()
                P.emit("dve", (lambda t, f: lambda e: e.tensor_reduce(out=ss[f], in_=ssq[t], op=ALU.add,
                                                                      axis=mybir.AxisListType.X))(t, f), [RSQ[t]], [Rss[f]])
                rstd_chain(rs[f], ss[f], 1.0 / D, 1e-6, Rss[f], Rrs[f])
                stt(X3[t], X3[t], rs[f][:, 0:1], fw, ALU.mult, ALU.mult, [RX3[t], Rrs[f], Rfw], [RX3[t]])
                r0 = (g * GT + t) * 128
                dma("sp", out_d[r0:r0 + 128, :], X3[t], "do%d" % t, reads=[RX3[t]])
        P.barrier()
        A.release(m0)

    ffn_down()
    return finish()


def host_consts(cfg):
    c = cfg
    ident = np.eye(128, dtype=np.float32)
    Rm = np.zeros((128, 128), np.float32)
    for d in range(64):
        Rm[d + 64, d] = -1.0
        Rm[d, d + 64] = 1.0
    ones = np.ones((128, 128), np.float32)
    cst = np.concatenate([ident, Rm, ones], axis=1)
    return np.ascontiguousarray(cst)


def rope_tables_T(positions):
    inv = (1.0 / (np.float32(10000.0) ** (np.arange(0, 128, 2, dtype=np.float32) / np.float32(128)))).astype(np.float32)
    ang = positions.astype(np.float32)[:, None] * inv[None, :]
    ang = np.concatenate([ang, ang], axis=-1)
    return np.ascontiguousarray(np.cos(ang).astype(np.float32).T), np.ascontiguousarray(np.sin(ang).astype(np.float32).T)


def band_tables(cfg, rel_bias):
    i = np.arange(128)[:, None]; j = np.arange(128)[None, :]
    tabs = []
    for dd in range(5):
        rel = np.clip(128 * (dd - 4) + i - j, -256, 256) + 256
        tabs.append(rel)
    idx = np.stack(tabs, 0)
    g = rel_bias[:, idx]
    g = np.transpose(g, (2, 0, 1, 3))
    return np.ascontiguousarray(g.reshape(128, -1).astype(np.float32))


def col_layout(v):
    return np.ascontiguousarray(v.reshape(-1, 128).T)


def make_in_maps(cfg, x, mem, norm_mix_w, w_in, lambda_q1, lambda_k1, lambda_q2, lambda_k2, subln_w, rel_bias,
                 w_out, norm_xq_w, norm_mem_w, w_mq, w_mkv, w_mo, norm_ffn_w, w_up, conv_w, conv_b, w_down,
                 final_norm_w):
    c = cfg
    f = lambda a: np.ascontiguousarray(np.asarray(a, dtype=np.float32))
    x = f(x); mem = f(mem)
    S = x.shape[1]
    half = S // 2
    assert half == c.TO * 128 == c.TP * 128
    shared = {
        "w_in": f(w_in[0]), "w_out": f(w_out[0]), "w_mq": f(w_mq[0]), "w_mkv": f(w_mkv[0]), "w_mo": f(w_mo[0]),
        "w_up": f(w_up[0]), "w_down": f(w_down[0]),
        "ncols": np.ascontiguousarray(np.concatenate([col_layout(f(norm_mix_w[0])), col_layout(f(norm_xq_w[0])),
                                                      col_layout(f(norm_mem_w[0])), col_layout(f(norm_ffn_w[0]))], axis=1)),
        "fnw": np.ascontiguousarray(np.broadcast_to(f(final_norm_w)[None, :], (128, c.D))),
        "cst": host_consts(c),
        "lamp": np.ascontiguousarray(np.broadcast_to(
            np.concatenate([f(lambda_q1[0]), f(lambda_k1[0]), f(lambda_q2[0]), f(lambda_k2[0])])[None, :], (128, 512))),
        "sublnb": np.ascontiguousarray(np.broadcast_to(f(subln_w[0])[None, :], (128, 256))),
        "cw": np.ascontiguousarray(np.stack([col_layout(f(conv_w[0][0])), col_layout(f(conv_w[0][1])),
                                             col_layout(f(conv_w[0][2])), col_layout(f(conv_b[0]))], axis=2).reshape(128, -1)),
        "bt": band_tables(c, f(rel_bias[0])),
    }
    maps = []
    for b in range(x.shape[0]):
        for h in range(2):
            if h == 0:
                xl = np.concatenate([np.zeros((half, c.D), np.float32), x[b, :half]], axis=0)
                pos = np.concatenate([np.zeros(half), np.arange(half)])
            else:
                xl = x[b]
                pos = np.arange(S)
            cosT, sinT = rope_tables_T(pos)
            m = dict(shared)
            m.update({"xl": np.ascontiguousarray(xl), "memb": mem[b], "cosT": cosT, "sinT": sinT,
                      "flagc": np.full((128, 1), float(h), np.float32)})
            maps.append(m)
    return maps


_CACHE = {}


def kernel(**inputs):
    cfg = Cfg()
    if "nc" not in _CACHE:
        _CACHE["nc"] = build(cfg)[0]
    nc = _CACHE["nc"]
    maps = make_in_maps(cfg, **inputs)
    n = len(maps)
    res = run_bass_kernel_spmd(nc, maps, core_ids=list(range(n)))
    B = inputs["x"].shape[0]
    out = np.empty((B, 2 * cfg.NO, cfg.D), np.float32)
    for b in range(B):
        for h in range(2):
            out[b, h * cfg.NO:(h + 1) * cfg.NO] = np.asarray(res.results[2 * b + h]["out"], dtype=np.float32)
    return out
```

```python
import math
import numpy as np
import concourse.bass as bass
import concourse.mybir as mybir
from concourse.bass_utils import run_bass_kernel_spmd

F32 = mybir.dt.float32
BF16 = mybir.dt.bfloat16
AF = mybir.ActivationFunctionType
ALU = mybir.AluOpType
ENGS = ("sp", "act", "pool", "pe", "dve")


class Cfg:
    def __init__(self, D=4096, WA=2048, WB=2048, NMH=4, MEM=256, DFF=11008, TP=16, TO=16, BATCH=4):
        self.D = D; self.KC = D // 128
        self.WA = WA; self.WB = WB
        self.NDH = WA // 256; self.NQH = 2 * self.NDH; self.NBH = WB // 128
        self.NMH = NMH; self.MHD = D // NMH; self.MEM = MEM
        self.DFF = DFF; self.NFB = DFF // 128
        self.TP = TP; self.TO = TO; self.NLT = TP + TO; self.L = self.NLT * 128
        self.QT0 = TP - 1; self.NQT = TO + 1; self.NQ = self.NQT * 128
        self.KB0 = self.QT0 - 4
        self.NO = TO * 128
        self.BATCH = BATCH
        self.INC = 3 * WA + 3 * WB
        assert WA + WB == D and self.KC % 4 == 0 and self.KB0 >= 0


class Res:
    __slots__ = ("w", "r", "rd", "excl")

    def __init__(self):
        self.w = None; self.r = {}; self.rd = []; self.excl = False


class Prog:
    def __init__(self):
        self.ops = []
        self.q = {e: [] for e in ENGS}
        self.res = []
        self.pending = {e: set() for e in ENGS}
        self.last_c = {e: None for e in ENGS}
        self.last_d = {}
        self.bg = set()

    def R(self):
        r = Res(); self.res.append(r); return r

    def Rs(self, n):
        return [self.R() for _ in range(n)]

    def emit(self, eng, fn, reads=(), writes=(), dsem=None):
        idx = len(self.ops)
        deps = set(self.pending[eng]); self.pending[eng] = set()
        for r in reads:
            if r.w is not None: deps.add(r.w)
            if r.excl:
                deps.update(v for k_, v in r.r.items() if k_ != eng)
        for w in writes:
            if w.w is not None: deps.add(w.w)
            deps.update(w.r.values()); deps.update(w.rd)
        isd = dsem is not None
        for r in reads:
            if isd: r.rd.append(idx)
            else: r.r[eng] = idx
        for w in writes:
            w.w = idx; w.r = {}; w.rd = []
        if eng == "pe":
            deps = {d for d in deps if not (self.ops[d][0] == "pe" and self.ops[d][3] is None)}
        self.ops.append([eng, fn, deps, dsem, False])
        self.q[eng].append(idx)
        if isd: self.last_d[dsem] = idx
        else: self.last_c[eng] = idx
        return idx

    def barrier(self):
        toks = set()
        for e in ENGS:
            if self.last_c[e] is not None: toks.add(self.last_c[e])
        toks.update(v for k_, v in self.last_d.items() if k_ not in self.bg)
        for e in ENGS:
            self.pending[e] |= toks
        for r in self.res:
            r.w = None; r.r = {}; r.rd = []

    def join_bg(self, eng):
        self.pending[eng] |= {v for k_, v in self.last_d.items() if k_ in self.bg}
        self.bg = set()

    def replay(self, nc, stack):
        need = set()
        for op in self.ops: need |= op[2]
        for e in ENGS: need |= self.pending[e]
        dnames = sorted({op[3] for op in self.ops if op[3] is not None})
        sem = {}
        for nm in list(ENGS) + ["d_" + d for d in dnames]:
            sem[nm] = stack.enter_context(nc.semaphore("s_" + nm))
        cnt = {k: 0 for k in sem}
        tok = {}
        for idx, op in enumerate(self.ops):
            if op[3] is not None:
                k = "d_" + op[3]; cnt[k] += 16; tok[idx] = (k, cnt[k])
            elif idx in need:
                cnt[op[0]] += 1; tok[idx] = (op[0], cnt[op[0]]); op[4] = True
        self.maxcnt = dict(cnt)
        block = stack.enter_context(nc.Block())

        def run(eng, e):
            waited = {}
            def dowaits(deps):
                best = {}
                for d in deps:
                    k, v = tok[d]
                    if best.get(k, 0) < v: best[k] = v
                for k, v in sorted(best.items()):
                    if waited.get(k, 0) < v:
                        e.wait_ge(sem[k], v); waited[k] = v
            for idx in self.q[eng]:
                _, fn, deps, dsem, sig = self.ops[idx]
                dowaits(deps)
                ins = fn(e)
                if dsem is not None: ins.then_inc(sem["d_" + dsem], 16)
                elif sig: ins.then_inc(sem[eng], 1)
            dowaits(self.pending[eng])

        @block.sync
        def _(e): run("sp", e)

        @block.scalar
        def _(e): run("act", e)

        @block.gpsimd
        def _(e): run("pool", e)

        @block.tensor
        def _(e): run("pe", e)

        @block.vector
        def _(e): run("dve", e)


class Arena:
    def __init__(self, ap, nbytes):
        self.ap = ap; self.n = nbytes; self.off = 0

    def mark(self): return self.off

    def release(self, m): self.off = m

    def _alloc(self, nb):
        o = (self.off + 63) // 64 * 64
        assert o + nb <= self.n, ("arena overflow", o, nb, self.n)
        self.off = o + nb
        return o

    def f32(self, n):
        o = self._alloc(4 * n); return self.ap[:, o // 4: o // 4 + n]

    def bf16(self, n):
        n2 = (n + 1) // 2 * 2
        o = self._alloc(2 * n2); return self.ap[:, o // 4: o // 4 + n2 // 2].bitcast(BF16)[:, 0:n]


def ring(n):
    st = {"i": -1}
    def nxt():
        st["i"] += 1; return st["i"] % n
    return nxt


def build(cfg, debug=False, upto=99):
    c = cfg
    D, KC, L, NQ, NO = c.D, c.KC, c.L, c.NQ, c.NO
    nc = bass.Bass("TRN2", target_bir_lowering=False)
    outs = []

    def din(name, shape, dt=F32):
        return nc.dram_tensor(name, list(shape), dt, kind="ExternalInput").ap()

    def dscr(name, shape, dt):
        if debug: outs.append(name)
        return nc.dram_tensor(name, list(shape), dt, kind=("ExternalOutput" if debug else "Internal")).ap()

    xl = din("xl", [L, D]); memb = din("memb", [c.MEM, D])
    w_in = din("w_in", [D, c.INC]); w_out = din("w_out", [D, D]); w_mq = din("w_mq", [D, D])
    w_mkv = din("w_mkv", [D, 2 * D]); w_mo = din("w_mo", [D, D]); w_up = din("w_up", [D, 2 * c.DFF])
    w_down = din("w_down", [c.DFF, D])
    ncols_d = din("ncols", [128, 4 * KC]); fnw_d = din("fnw", [128, D])
    cosT = din("cosT", [128, L]); sinT = din("sinT", [128, L])
    cst_d = din("cst", [128, 3 * 128])
    lam_d = din("lamp", [128, 4 * 128]); subln_d = din("sublnb", [128, 256]); flag_d = din("flagc", [128, 1])
    cw_d = din("cw", [128, c.NFB * 4]); bt_d = din("bt", [128, c.NBH * 5 * 128])
    out_d = nc.dram_tensor("out", [NO, D], F32, kind="ExternalOutput").ap()

    QTa = dscr("QTa", [c.NQH * 128, NQ], BF16); KTa = dscr("KTa", [c.NQH * 128, L], BF16)
    Va = dscr("Va", [L, c.WA], BF16)
    QTb = dscr("QTb", [c.WB, NQ], BF16); KTb = dscr("KTb", [c.WB, L], BF16); Vb = dscr("Vb", [L, c.WB], BF16)
    ATmix = dscr("ATmix", [D, NQ], BF16)
    X1 = dscr("X1", [NQ, D], F32); X2 = dscr("X2", [NQ, D], F32)
    QmT = dscr("QmT", [D, NQ], BF16); KmT = dscr("KmT", [D, c.MEM], BF16); Vm = dscr("Vm", [c.MEM, D], BF16)
    ATmo = dscr("ATmo", [D, NQ], BF16)
    ACTT = dscr("ACTT", [c.DFF, NO], BF16)
    WDb = dscr("WDb", [c.DFF, D], BF16)

    from contextlib import ExitStack
    stack = ExitStack()
    ARENA_B = 200 * 1024
    arena_t = stack.enter_context(nc.sbuf_tensor("arena", [128, ARENA_B // 4], F32))
    psum = stack.enter_context(nc.psum_tensor("psum", [128, 8, 512], F32))
    A = Arena(arena_t, ARENA_B)
    P = Prog()
    bank = [psum[:, i, :] for i in range(8)]
    Rbank = P.Rs(8)
    for r_ in Rbank: r_.excl = True

    def finish():
        P.barrier()
        P.replay(nc, stack)
        stack.close()
        return nc, outs, P

    def dma(q, out, in_, dsem, reads=(), writes=()):
        P.emit(q, lambda e: e.dma_start(out=out, in_=in_), reads, writes, dsem=dsem)

    def mm(out, lhsT, rhs, start, stop, reads, writes):
        P.emit("pe", lambda e: e.matmul(out, lhsT=lhsT, rhs=rhs, start=start, stop=stop), reads, writes)

    def tr(out, in_, ident, reads, writes):
        P.emit("pe", lambda e: e.transpose(out, in_, ident), reads, writes)

    def act(out, in_, func, reads, writes, scale=None, bias=None, accum_out=None):
        kw = {}
        if scale is not None: kw["scale"] = scale
        if bias is not None: kw["bias"] = bias
        if accum_out is not None: kw["accum_out"] = accum_out
        P.emit("act", lambda e: e.activation(out=out, in_=in_, func=func, **kw), reads, writes)

    def ts(eng, out, in0, s1, s2, op0, op1, reads, writes):
        if op1 is None:
            P.emit(eng, lambda e: e.tensor_scalar(out=out, in0=in0, scalar1=s1, scalar2=None, op0=op0), reads, writes)
        else:
            P.emit(eng, lambda e: e.tensor_scalar(out=out, in0=in0, scalar1=s1, scalar2=s2, op0=op0, op1=op1), reads, writes)

    def tt(eng, out, in0, in1, op, reads, writes):
        P.emit(eng, lambda e: e.tensor_tensor(out=out, in0=in0, in1=in1, op=op), reads, writes)

    def stt(out, in0, scalar, in1, op0, op1, reads, writes):
        P.emit("dve", lambda e: e.scalar_tensor_tensor(out=out, in0=in0, scalar=scalar, in1=in1, op0=op0, op1=op1), reads, writes)

    def recip(out, in_, reads, writes):
        P.emit("dve", lambda e: e.reciprocal(out=out, in_=in_), reads, writes)

    def memset(eng, ap, val, reads, writes):
        P.emit(eng, lambda e: e.memset(ap, val), reads, writes)

    def cpy(eng, out, in_, reads, writes):
        if eng == "act":
            act(out, in_, AF.Copy, reads, writes)
        else:
            P.emit(eng, lambda e: e.tensor_copy(out=out, in_=in_), reads, writes)

    def rstd_chain(rs, ss, inv_n, eps, Rss, Rrs, post=1.0):
        act(rs, ss, AF.Sqrt, [Rss], [Rrs], scale=inv_n / (post * post), bias=epsb(eps / (post * post)))
        recip(rs, rs, [Rrs], [Rrs])

    ident_f = A.f32(128); ones_f = A.f32(128)
    ident_b = A.bf16(128); Rm_b = A.bf16(128); ones_b = A.bf16(128)
    ncols = A.f32(4 * KC); flag = A.f32(1); lamw = A.f32(512); lamc = A.f32(4); neglam = A.f32(1)
    Rc = P.R()
    eps_tiles = {}
    for ev_ in (1e-6, 1e-6 / ((1.0 - (0.8 - 0.6 * math.exp(0.0))) ** 2)):
        t_ = A.f32(1); eps_tiles[ev_] = t_
        memset("dve", t_, ev_, [], [Rc])

    def epsb(v):
        for k_, t_ in eps_tiles.items():
            if abs(k_ - v) <= 1e-12 * max(1.0, abs(v)): return t_
        raise KeyError(v)
    dma("sp", ident_f, cst_d[:, 0:128], "c0", writes=[Rc])
    dma("sp", ones_f, cst_d[:, 256:384], "c0", writes=[Rc])
    dma("sp", ncols, ncols_d, "c0", writes=[Rc])
    dma("sp", flag, flag_d, "c0", writes=[Rc])
    dma("sp", lamw, lam_d, "c0", writes=[Rc])
    dma("pool", ident_b, cst_d[:, 0:128], "c1", writes=[Rc])
    dma("pool", Rm_b, cst_d[:, 128:256], "c1", writes=[Rc])
    dma("pool", ones_b, cst_d[:, 256:384], "c1", writes=[Rc])
    lam_init = 0.8 - 0.6 * math.exp(0.0)
    ljunk = A.f32(128)
    for k in range(2):
        P.emit("dve", (lambda k: lambda e: e.tensor_tensor(out=ljunk, in0=lamw[:, 256 * k:256 * k + 128],
                                                            in1=lamw[:, 256 * k + 128:256 * k + 256], op=ALU.mult))(k),
               [Rc], [Rc])
        P.emit("dve", (lambda k: lambda e: e.tensor_reduce(out=lamc[:, k:k + 1], in_=ljunk, op=ALU.add,
                                                           axis=mybir.AxisListType.X))(k), [Rc], [Rc])
    act(lamc[:, 0:2], lamc[:, 0:2], AF.Exp, [Rc], [Rc])
    tt("dve", lamc[:, 2:3], lamc[:, 1:2], lamc[:, 0:1], ALU.subtract, [Rc], [Rc])
    ts("dve", neglam, lamc[:, 2:3], -lam_init, None, ALU.add, None, [Rc], [Rc])
    P.barrier()
    PM = A.mark()

    def wv(w):
        return w.rearrange("(kc p) n -> p kc n", p=128)

    def norm_phase(srcs, wcol, AT3, eps=1e-6, NX=3):
        m0 = A.mark()
        xt = [A.f32(D) for _ in range(NX)]; junk = A.bf16(D)
        ss = [A.f32(1) for _ in range(NX)]; rs = [A.f32(1) for _ in range(NX)]
        Rxt = P.Rs(NX); Rss = P.Rs(NX); Rrs = P.Rs(NX); Rjk = P.R()
        tbr = ring(4); alt = ring(2)
        def stA(i, src):
            s = i % NX
            dma("sp", xt[s], src, "xt%d" % s, writes=[Rxt[s]])
            act(junk, xt[s], AF.Square, [Rxt[s]], [Rss[s], Rjk], accum_out=ss[s])
            rstd_chain(rs[s], ss[s], 1.0 / D, eps, Rss[s], Rrs[s])
            ts("dve", xt[s], xt[s], rs[s], None, ALU.mult, None, [Rxt[s], Rrs[s]], [Rxt[s]])

        gcnt = ring(8)

        def stB(i):
            s = i % NX
            for g in range(KC // 4):
                b = 4 + tbr()
                for q in range(4):
                    kc = g * 4 + q
                    tr(bank[b][:, q * 128:(q + 1) * 128], xt[s][:, kc * 128:(kc + 1) * 128], ident_f, [Rxt[s]], [Rbank[b]])
                on_act = gcnt() in (0, 3, 6)
                for q in range(4):
                    kc = g * 4 + q
                    o = AT3[:, kc, i * 128:(i + 1) * 128]; src_ = bank[b][:, q * 128:(q + 1) * 128]
                    if on_act:
                        act(o, src_, AF.Copy, [Rbank[b]], [], scale=wcol[:, kc:kc + 1])
                    else:
                        ts("dve", o, src_, wcol[:, kc:kc + 1], None, ALU.mult, None, [Rbank[b]], [])
        n_ = len(srcs)
        stA(0, srcs[0])
        for i in range(n_):
            if i + 1 < n_: stA(i + 1, srcs[i + 1])
            stB(i)
        P.barrier()
        A.release(m0)

    def load_AT(src, AT3, ntok):
        Rl = P.R()
        v = src.rearrange("(kc p) n -> p kc n", p=128)
        nk = v.shape[1]
        step = max(1, nk // 4)
        for k0 in range(0, nk, step):
            k1 = min(nk, k0 + step)
            dma("sp", AT3[:, k0:k1, 0:ntok], v[:, k0:k1, :], "lat", writes=[Rl])
        P.barrier()

    gb = ring(4)

    def gemm(jobs, AT3, KCn, CB, WSLOTS=2, jobs2=None, ratio=1, every=1):
        m0 = A.mark()
        wslot = [A.bf16(KCn * CB).rearrange("p (k c) -> p k c", c=CB) for _ in range(WSLOTS)]
        Rw = P.Rs(WSLOTS)
        wr = ring(WSLOTS)
        def mkitems(jobs_):
            its_ = []
            for jb in jobs_:
                for b in range(jb["ncols"] // CB):
                    c0 = jb["col0"] + b * CB
                    if jb["form"] == "F":
                        sub = [(m, t0, n) for m in range(CB // 128) for (t0, n) in jb["toks"]]
                    else:
                        sub = [(j,) for j in jb["tiles"]]
                    its_.append((jb, c0, sub))
            return its_
        items = mkitems(jobs)
        if jobs2:
            it2 = mkitems(jobs2); merged = []
            for n_it, it_ in enumerate(items):
                merged.append(it_)
                if (n_it + 1) % every == 0:
                    for _ in range(ratio):
                        if it2: merged.append(it2.pop(0))
            items = merged + it2
        flat = [(jb, c0, su) for (jb, c0, sub) in items for su in sub]
        k = 0
        lastk = {}
        for k_, it_ in enumerate(flat): lastk[id(it_[0])] = k_
        if flat: flat[0][0]["pre"](0, flat[0])

        def loadW(bi):
            jb_, c0_, _ = items[bi]
            s_ = wr()
            dma("pool", wslot[s_], jb_["Wv"][:, :, c0_:c0_ + CB], "w%d" % s_, writes=[Rw[s_]])
            return s_
        wsl = {}
        for bi in range(min(WSLOTS - 1, len(items))): wsl[bi] = loadW(bi)
        for bi, (jb, c0, sub) in enumerate(items):
            nb_ = bi + WSLOTS - 1
            if nb_ < len(items): wsl[nb_] = loadW(nb_)
            s = wsl[bi]
            for su in sub:
                b = gb()
                ATj = jb.get("AT", AT3)
                if jb["form"] == "F":
                    m, t0, n = su
                    for kc in range(KCn):
                        mm(bank[b][:, 0:n], wslot[s][:, kc, m * 128:(m + 1) * 128], ATj[:, kc, t0:t0 + n],
                           kc == 0, kc == KCn - 1, [Rw[s]], [Rbank[b]])
                else:
                    (j,) = su
                    for kc in range(KCn):
                        mm(bank[b][:, 0:CB], ATj[:, kc, j * 128:(j + 1) * 128], wslot[s][:, kc, :],
                           kc == 0, kc == KCn - 1, [Rw[s]], [Rbank[b]])
                if k + 1 < len(flat): flat[k + 1][0]["pre"](k + 1, flat[k + 1])
                jb["epi"](k, (jb, c0, su), b)
                k += 1
                if (k == len(flat) or flat[k][0] is not jb) and "flush" in jb: jb["flush"]()
        P.barrier()
        A.release(m0)

    def nopre(k, item): pass

    def rope_bufs():
        return dict(cs=[A.f32(1024) for _ in range(3)], tb=[A.bf16(512) for _ in range(2)],
                    ta=[A.f32(512) for _ in range(2)], tb2=[A.f32(512) for _ in range(2)], ob=[A.bf16(512) for _ in range(2)],
                    Rcs=P.Rs(3), Rtb=P.Rs(2), Rta=P.Rs(2), Rtb2=P.Rs(2), Rob=P.Rs(2), rb=ring(2))

    def mk_epi_rope(B_, dest, dcol_of, ltok_of, colbase):
        cs, tb, ta, tb2, ob = B_["cs"], B_["tb"], B_["ta"], B_["tb2"], B_["ob"]
        Rcs, Rtb, Rta, Rtb2, Rob, rb = B_["Rcs"], B_["Rtb"], B_["Rta"], B_["Rtb2"], B_["Rob"], B_["rb"]

        def pre(k, item):
            jb, c0, (m, t0, n) = item
            s = k % 3; lt = ltok_of(t0)
            dma("sp", cs[s][:, 0:n], cosT[:, lt:lt + n], "cs%d" % s, writes=[Rcs[s]])
            dma("sp", cs[s][:, 512:512 + n], sinT[:, lt:lt + n], "cs%d" % s, writes=[Rcs[s]])

        pend = []

        def flush():
            while pend: pend.pop(0)()

        def epi(k, item, b):
            jb, c0, (m, t0, n) = item
            s = k % 2; s3 = k % 3; hb = (c0 - colbase) // 128 + m
            act(tb[s][:, 0:n], bank[b][:, 0:n], AF.Copy, [Rbank[b]], [Rtb[s]])
            tt("dve", ta[s][:, 0:n], bank[b][:, 0:n], cs[s3][:, 0:n], ALU.mult, [Rbank[b], Rcs[s3]], [Rta[s]])
            flush()

            def phase2():
                r = 6 + rb()
                mm(bank[r][:, 0:n], Rm_b, tb[s][:, 0:n], True, True, [Rtb[s]], [Rbank[r]])
                tt("dve", tb2[s][:, 0:n], bank[r][:, 0:n], cs[s3][:, 512:512 + n], ALU.mult, [Rbank[r], Rcs[s3]], [Rtb2[s]])
                tt("dve", ob[s][:, 0:n], ta[s][:, 0:n], tb2[s][:, 0:n], ALU.add, [Rta[s], Rtb2[s]], [Rob[s]])
                dc = dcol_of(t0)
                dma("sp", dest[hb * 128:(hb + 1) * 128, dc:dc + n], ob[s][:, 0:n], "ob%d" % s, reads=[Rob[s]])
            pend.append(phase2)
        epi.flush = flush
        return pre, epi

    def mk_epi_plainF(dest, dcol_of, colbase, nm):
        ob = [A.bf16(512) for _ in range(2)]; Rob = P.Rs(2); alt = ring(2)

        def epi(k, item, b):
            jb, c0, (m, t0, n) = item
            s = k % 2; hb = (c0 - colbase) // 128 + m
            cpy("act" if alt() == 0 else "dve", ob[s][:, 0:n], bank[b][:, 0:n], [Rbank[b]], [Rob[s]])
            dc = dcol_of(t0)
            dma("sp", dest[hb * 128:(hb + 1) * 128, dc:dc + n], ob[s][:, 0:n], nm + "%d" % s, reads=[Rob[s]])
        return nopre, epi

    def mk_epi_V(dest, drow_of, colbase, CB, use_flag, nm):
        ob = [A.bf16(CB) for _ in range(2)]; Rob = P.Rs(2)

        def epi(k, item, b):
            jb, c0, (j,) = item
            s = k % 2
            if use_flag:
                act(ob[s], bank[b][:, 0:CB], AF.Copy, [Rbank[b]], [Rob[s]], scale=flag[:, 0:1])
            else:
                cpy("act", ob[s], bank[b][:, 0:CB], [Rbank[b]], [Rob[s]])
            r0 = drow_of(j)
            dma("sp", dest[r0:r0 + 128, c0 - colbase:c0 - colbase + CB], ob[s], nm + "%d" % s, reads=[Rob[s]])
        return nopre, epi

    def mk_epi_resid(src, srow_of, dest, drow_of, CB):
        xs = [A.f32(CB) for _ in range(2)]; ob = [A.f32(CB) for _ in range(2)]
        Rxs = P.Rs(2); Rob = P.Rs(2)

        def pre(k, item):
            jb, c0, (j,) = item
            s = k % 2; r0 = srow_of(j)
            dma("sp", xs[s], src[r0:r0 + 128, c0:c0 + CB], "xs%d" % s, writes=[Rxs[s]])

        def epi(k, item, b):
            jb, c0, (j,) = item
            s = k % 2
            tt("dve", ob[s], bank[b][:, 0:CB], xs[s], ALU.add, [Rbank[b], Rxs[s]], [Rob[s]])
            r0 = drow_of(j)
            dma("sp", dest[r0:r0 + 128, c0:c0 + CB], ob[s], "ro%d" % s, reads=[Rob[s]])
        return pre, epi

    def chunks(t0, t1):
        r = []
        while t0 < t1:
            n = min(512, t1 - t0); r.append((t0, n)); t0 += n
        return r

    CBW = 256
    Win = wv(w_in)
    WA, WB = c.WA, c.WB
    cq_a, ck_a, cv_a = 0, WA, 2 * WA
    cq_b, ck_b, cv_b = 3 * WA, 3 * WA + WB, 3 * WA + 2 * WB
    def alloc_AT(ntok):
        return A.bf16(KC * ntok).rearrange("p (k n) -> p k n", n=ntok)

    if upto < 1:
        return finish()
    TP = c.TP
    AT3 = alloc_AT(TP * 128)
    import os
    if os.environ.get("K_NONORM") != "1":
      norm_phase([xl[i * 128:(i + 1) * 128, :] for i in range(TP)], ncols[:, 0:KC], AT3)
    pre_r, epi_r = mk_epi_rope(rope_bufs(), KTa, lambda t0: t0, lambda t0: t0, ck_a)
    _, epi_va = mk_epi_V(Va, lambda j: j * 128, cv_a, CBW, True, "va")
    _, epi_kb = mk_epi_plainF(KTb, lambda t0: t0, ck_b, "kb")
    _, epi_vb = mk_epi_V(Vb, lambda j: j * 128, cv_b, CBW, True, "vb")
    JSEL = os.environ.get("K_JOBS", "0123")
    if upto == 1:
        gemm([jj for ii, jj in enumerate([
        dict(form="F", Wv=Win, col0=ck_a, ncols=WA, toks=chunks(0, TP * 128), pre=pre_r, epi=epi_r, flush=epi_r.flush),
        dict(form="T", Wv=Win, col0=cv_a, ncols=WA, tiles=list(range(TP)), pre=nopre, epi=epi_va),
        dict(form="F", Wv=Win, col0=ck_b, ncols=WB, toks=chunks(c.KB0 * 128, TP * 128), pre=nopre, epi=epi_kb),
        dict(form="T", Wv=Win, col0=cv_b, ncols=WB, tiles=list(range(c.KB0, TP)), pre=nopre, epi=epi_vb)]) if str(ii) in JSEL], AT3, KC, CBW)
        return finish()
    gemm([
        dict(form="F", Wv=Win, col0=ck_a, ncols=WA, toks=chunks(0, TP * 128), pre=pre_r, epi=epi_r, flush=epi_r.flush),
        dict(form="T", Wv=Win, col0=cv_a, ncols=WA, tiles=list(range(TP)), pre=nopre, epi=epi_va),
        dict(form="F", Wv=Win, col0=ck_b, ncols=WB, toks=chunks(c.KB0 * 128, TP * 128), pre=nopre, epi=epi_kb),
        dict(form="T", Wv=Win, col0=cv_b, ncols=WB, tiles=list(range(c.KB0, TP)), pre=nopre, epi=epi_vb),
    ], AT3, KC, CBW)
    A.release(PM)

    if upto < 2:
        return finish()
    QT0 = c.QT0
    qbase = QT0 * 128
    TH = c.TO // 2
    MEM = c.MEM
    ATm = alloc_AT(MEM)
    norm_phase([memb[i * 128:(i + 1) * 128, :] for i in range(MEM // 128)], ncols[:, 2 * KC:3 * KC], ATm)
    PMS = A.mark()
    for (qa0, qa1) in ((0, TH + 1), (TH + 1, c.NQT)):
        nt = qa1 - qa0
        tb0 = qa0 * 128
        if qa0 == 0:
            _, epi_mk = mk_epi_plainF(KmT, lambda t0: t0, 0, "mk")
            jobs2 = [dict(form="F", Wv=wv(w_mkv), col0=0, ncols=D, toks=chunks(0, MEM), pre=nopre, epi=epi_mk, AT=ATm)]
        else:
            _, epi_mv = mk_epi_V(Vm, lambda j: j * 128, D, CBW, False, "mv")
            jobs2 = [dict(form="T", Wv=wv(w_mkv), col0=D, ncols=D, tiles=list(range(MEM // 128)), pre=nopre, epi=epi_mv, AT=ATm)]
        AT3 = alloc_AT(nt * 128)
        norm_phase([xl[(QT0 + qa0 + i) * 128:(QT0 + qa0 + i + 1) * 128, :] for i in range(nt)], ncols[:, 0:KC], AT3)
        RB = rope_bufs()
        pre_q, epi_q = mk_epi_rope(RB, QTa, lambda t0, tb0=tb0: tb0 + t0, lambda t0, tb0=tb0: qbase + tb0 + t0, cq_a)
        pre_k, epi_k = mk_epi_rope(RB, KTa, lambda t0, tb0=tb0: qbase + tb0 + t0, lambda t0, tb0=tb0: qbase + tb0 + t0, ck_a)
        _, epi_va = mk_epi_V(Va, lambda j, qa0=qa0: (QT0 + qa0 + j) * 128, cv_a, CBW, False, "va")
        _, epi_qb = mk_epi_plainF(QTb, lambda t0, tb0=tb0: tb0 + t0, cq_b, "qb")
        _, epi_kb = mk_epi_plainF(KTb, lambda t0, tb0=tb0: qbase + tb0 + t0, ck_b, "kb")
        _, epi_vb = mk_epi_V(Vb, lambda j, qa0=qa0: (QT0 + qa0 + j) * 128, cv_b, CBW, False, "vb")
        k0t = 1 if qa0 == 0 else 0
        own = list(range(k0t, nt))
        gemm([
            dict(form="F", Wv=Win, col0=cq_a, ncols=WA, toks=chunks(0, nt * 128), pre=pre_q, epi=epi_q, flush=epi_q.flush),
            dict(form="F", Wv=Win, col0=ck_a, ncols=WA, toks=chunks(k0t * 128, nt * 128), pre=pre_k, epi=epi_k, flush=epi_k.flush),
            dict(form="T", Wv=Win, col0=cv_a, ncols=WA, tiles=own, pre=nopre, epi=epi_va),
            dict(form="F", Wv=Win, col0=cq_b, ncols=WB, toks=chunks(0, nt * 128), pre=nopre, epi=epi_qb),
            dict(form="F", Wv=Win, col0=ck_b, ncols=WB, toks=chunks(k0t * 128, nt * 128), pre=nopre, epi=epi_kb),
            dict(form="T", Wv=Win, col0=cv_b, ncols=WB, tiles=own, pre=nopre, epi=epi_vb),
        ], AT3, KC, CBW, WSLOTS=4, jobs2=jobs2, ratio=1, every=3)
        A.release(PMS)
    A.release(PM)

    if upto < 3:
        return finish()
    NLT, NQT = c.NLT, c.NQT
    sc = 128 ** -0.5
    TINY = 1e-30

    def attn_A():
        m0 = A.mark()
        KT = [A.bf16(2 * L).rearrange("p (m l) -> p m l", m=2) for _ in range(2)]
        QT = [A.bf16(2 * NQ).rearrange("p (m l) -> p m l", m=2) for _ in range(2)]
        V = [A.bf16(NLT * 258).rearrange("p (t d) -> p t d", d=258) for _ in range(2)]
        oT = [A.bf16(2 * NQ).rearrange("p (m l) -> p m l", m=2) for _ in range(2)]
        ET = [A.bf16(512) for _ in range(4)]
        t0b = [A.f32(256) for _ in range(4)]; ob = [A.f32(256) for _ in range(4)]; onb = [A.f32(256) for _ in range(4)]
        sw = A.f32(256); junk = A.bf16(256)
        rr = [A.f32(2) for _ in range(4)]; c1 = [A.f32(1) for _ in range(4)]
        ss = [A.f32(1) for _ in range(4)]; rs = [A.f32(1) for _ in range(4)]
        RKT = P.Rs(2); RQT = P.Rs(2); RV = P.Rs(2); RoT = P.Rs(2); RET = P.Rs(4)
        Rt0 = P.Rs(4); Ro = P.Rs(4); Ron = P.Rs(4); Rrr = P.Rs(4); Rc1 = P.Rs(4); Rss = P.Rs(4); Rrs = P.Rs(4)
        Rsw = P.R(); Rjk = P.R()
        lnpost = A.f32(1)
        memset("dve", lnpost, math.log(1.0 - lam_init), [], [Rsw])
        dma("sp", sw, subln_d, "c0", writes=[Rsw])
        for s in range(2):
            memset("pool", V[s][:, :, 256:258], 1.0, [], [RV[s]])
            ts("dve", V[s][:, 0:TP, 256:257], V[s][:, 0:TP, 256:257], flag[:, 0:1], None, ALU.mult, None, [RV[s]], [RV[s]])
        sbr = ring(3); er = ring(4); tbr = ring(1)
        Vav = Va.rearrange("(t p) c -> p t c", p=128)

        def load_head(h):
            s = h % 2
            dma("sp", KT[s], KTa[2 * h * 128:(2 * h + 2) * 128, :].rearrange("(m p) l -> p m l", p=128), "ak%d" % s, writes=[RKT[s]])
            dma("sp", QT[s], QTa[2 * h * 128:(2 * h + 2) * 128, :].rearrange("(m p) l -> p m l", p=128), "aq%d" % s, writes=[RQT[s]])
            dma("sp", V[s][:, :, 0:256], Vav[:, :, h * 256:(h + 1) * 256], "av%d" % s, writes=[RV[s]])

        groups = []
        for h in range(c.NDH):
            for i in range(NQT):
                nk = QT0 + i + 1
                for g in range((nk + 1) // 2):
                    groups.append(dict(h=h, i=i, nk=nk, kts=[kt for kt in (2 * g, 2 * g + 1) if kt < nk],
                                       last=(g == (nk + 1) // 2 - 1)))

        def qk(G):
            s = G["h"] % 2; i = G["i"]
            sb = sbr(); G["sb"] = sb
            for j, kt in enumerate(G["kts"]):
                for m in range(2):
                    mm(bank[sb][:, (j * 2 + m) * 128:(j * 2 + m + 1) * 128], KT[s][:, m, kt * 128:(kt + 1) * 128],
                       QT[s][:, m, i * 128:(i + 1) * 128], True, True, [RKT[s], RQT[s]], [Rbank[sb]])

        def rest(G):
            h = G["h"]; s = h % 2; i = G["i"]; nk = G["nk"]; kts = G["kts"]; sb = G["sb"]; T = nk - 1
            Ob = [3 + (i % 2) * 2, 4 + (i % 2) * 2]
            ncol = len(kts) * 256
            e = er()
            act(ET[e][:, 0:ncol], bank[sb][:, 0:ncol], AF.Exp, [Rbank[sb]], [RET[e]], scale=sc)
            if T in kts:
                j = kts.index(T)
                dz = ET[e][64:128, j * 256:(j + 1) * 256].rearrange("p (m q) -> p m q", m=2)[:, :, 0:64]
                act(dz, dz, AF.Copy, [RET[e]], [RET[e]], scale=0.0)
            for j, kt in enumerate(kts):
                for m in range(2):
                    mm(bank[Ob[m]][:, 0:257], ET[e][:, (j * 2 + m) * 128:(j * 2 + m + 1) * 128], V[s][:, kt, 0:257],
                       kt == 0, kt == nk - 1, [RET[e], RV[s]], [Rbank[Ob[m]]])
            if not G["last"]: return
            G["fin"] = True

        def F1(G):
            i = G["i"]; f = i % 4
            Ob = [3 + (i % 2) * 2, 4 + (i % 2) * 2]
            for m in range(2):
                ts("dve", rr[f][:, m:m + 1], bank[Ob[m]][:, 256:257], TINY, None, ALU.add, None, [Rbank[Ob[m]]], [Rrr[f]])
            recip(rr[f], rr[f], [Rrr[f]], [Rrr[f]])
            tt("dve", c1[f], rr[f][:, 1:2], neglam, ALU.mult, [Rrr[f]], [Rc1[f]])
            ts("dve", t0b[f], bank[Ob[0]][:, 0:256], rr[f][:, 0:1], None, ALU.mult, None, [Rbank[Ob[0]], Rrr[f]], [Rt0[f]])
            stt(ob[f], bank[Ob[1]][:, 0:256], c1[f][:, 0:1], t0b[f], ALU.mult, ALU.add, [Rbank[Ob[1]], Rc1[f], Rt0[f]], [Ro[f]])
            P.emit("dve", lambda e: e.scalar_tensor_tensor(out=junk, in0=ob[f], scalar=1.0, in1=ob[f], op0=ALU.mult, op1=ALU.mult,
                                                           accum_out=ss[f]), [Ro[f]], [Rss[f], Rjk])
            ts("dve", rs[f], ss[f], 1.0 / 256, 1e-6, ALU.mult, ALU.add, [Rss[f]], [Rrs[f]])

        def F2(G):
            i = G["i"]; f = i % 4
            act(rs[f], rs[f], AF.Ln, [Rrs[f]], [Rrs[f]])
            act(rs[f], rs[f], AF.Exp, [Rrs[f]], [Rrs[f]], scale=-0.5, bias=lnpost)
            stt(onb[f], ob[f], rs[f][:, 0:1], sw, ALU.mult, ALU.mult, [Ro[f], Rrs[f], Rsw], [Ron[f]])

        def fin_tr(G):
            h = G["h"]; s = h % 2; i = G["i"]; f = i % 4
            tb_ = 7 + tbr()
            for cc in range(2):
                tr(bank[tb_][:, cc * 128:(cc + 1) * 128], onb[f][:, cc * 128:(cc + 1) * 128], ident_f, [Ron[f]], [Rbank[tb_]])
            cpy("dve", oT[s][:, :, i * 128:(i + 1) * 128], bank[tb_][:, 0:256].rearrange("p (c q) -> p c q", c=2),
                [Rbank[tb_]], [RoT[s]])
            if i == NQT - 1:
                dma("sp", ATmix[h * 256:(h + 1) * 256, :].rearrange("(c p) n -> p c n", p=128), oT[s], "ao%d" % s, reads=[RoT[s]])

        load_head(0)
        if c.NDH > 1: load_head(1)
        P.bg.add("wc")
        rstep = 256 if c.DFF % 256 == 0 else 128
        casts = list(range(0, c.DFF, rstep))
        pace = [A.f32(1) for _ in range(4)]; Rpace = P.Rs(4)
        cst_i = [0, 0]

        def cast_step(paced):
            if not casts: return
            r0 = casts.pop(0)
            k_ = cst_i[0] % 4; cst_i[0] += 1
            rd = []
            if paced:
                memset("dve", pace[k_], 0.0, [], [Rpace[k_]])
                rd = [Rpace[k_]]
            dma("pool", WDb[r0:r0 + rstep, :], w_down[r0:r0 + rstep, :], "wc", reads=rd)
        LA = 2
        for n_ in range(min(LA, len(groups))): qk(groups[n_])
        sched = []

        def run_due(n_, force=False):
            while sched and (force or sched[0][0] <= n_):
                _, fn_, G_ = sched.pop(0)
                fn_(G_)
        for n_, G in enumerate(groups):
            if n_ + LA < len(groups):
                qk(groups[n_ + LA])
            rest(G)
            if G.get("fin"):
                F1(G)
                cst_i[1] += 1
                if cst_i[1] % 3 == 0: cast_step(True)
                sched.append((n_ + 6, F2, G))
                sched.append((n_ + 9, fin_tr, G))
                sched.sort(key=lambda t: t[0])
            run_due(n_)
            if G["last"] and G["i"] == NQT - 1:
                run_due(n_, force=True)
                if G["h"] + 2 < c.NDH: load_head(G["h"] + 2)
        while casts: cast_step(False)

        P.barrier()
        A.release(m0)

    attn_A()

    if upto < 4:
        return finish()
    def attn_B():
        m0 = A.mark()
        KB0 = c.KB0; NKT = NLT - KB0
        KT = [A.bf16(NKT * 128) for _ in range(2)]
        QT = [A.bf16(NQ) for _ in range(2)]
        V = [A.bf16(NKT * 130).rearrange("p (t d) -> p t d", d=130) for _ in range(2)]
        BT = [A.f32(5 * 128) for _ in range(2)]
        oT = [A.bf16(NQ) for _ in range(2)]
        SB = [A.f32(640) for _ in range(3)]
        ET = [A.bf16(640) for _ in range(4)]
        ob = [A.f32(128) for _ in range(4)]; rr = [A.f32(1) for _ in range(4)]
        RKT = P.Rs(2); RQT = P.Rs(2); RV = P.Rs(2); RBT = P.Rs(2); RoT = P.Rs(2); RSB = P.Rs(3); RET = P.Rs(4)
        Ro = P.Rs(4); Rrr = P.Rs(4)
        npv = TP - KB0
        for s in range(2):
            memset("pool", V[s][:, :, 128:130], 1.0, [], [RV[s]])
            ts("dve", V[s][:, 0:npv, 128:129], V[s][:, 0:npv, 128:129], flag[:, 0:1], None, ALU.mult, None, [RV[s]], [RV[s]])
        sbr = ring(4); er = ring(4); tbk = ring(4); fr = ring(4); sr = ring(3)
        Vbv = Vb.rearrange("(t p) c -> p t c", p=128)

        def load_head(h):
            s = h % 2
            dma("sp", KT[s], KTb[h * 128:(h + 1) * 128, KB0 * 128:L], "bk%d" % s, writes=[RKT[s]])
            dma("sp", QT[s], QTb[h * 128:(h + 1) * 128, :], "bq%d" % s, writes=[RQT[s]])
            dma("sp", V[s][:, :, 0:128], Vbv[:, KB0:NLT, h * 128:(h + 1) * 128], "bv%d" % s, writes=[RV[s]])
            dma("sp", BT[s], bt_d[:, h * 640:(h + 1) * 640], "bb%d" % s, writes=[RBT[s]])
            memset("pool", BT[s][64:128, 512:576], -30000.0, [], [RBT[s]])
            memset("pool", BT[s][0:64, 64:128], -30000.0, [], [RBT[s]])

        groups = [dict(h=h, i=i) for h in range(c.NBH) for i in range(NQT)]

        def qk(G):
            s = G["h"] % 2; i = G["i"]; T = QT0 + i
            sb = sbr(); G["sb"] = sb
            tbk_ = 4 + tbk(); G["tb"] = tbk_
            for dd in range(5):
                kt = T - 4 + dd - KB0
                dst = bank[sb][:, dd * 128:(dd + 1) * 128] if dd < 4 else bank[tbk_][:, 0:128]
                mm(dst, KT[s][:, kt * 128:(kt + 1) * 128], QT[s][:, i * 128:(i + 1) * 128], True, True,
                   [RKT[s], RQT[s]], [Rbank[sb] if dd < 4 else Rbank[tbk_]])

        def st1(G):
            h = G["h"]; s = h % 2; sb = G["sb"]; tbk_ = G["tb"]
            q = sr()
            stt(SB[q][:, 0:512], bank[sb][:, 0:512], sc, BT[s][:, 0:512], ALU.mult, ALU.add, [Rbank[sb], RBT[s]], [RSB[q]])
            stt(SB[q][:, 512:640], bank[tbk_][:, 0:128], sc, BT[s][:, 512:640], ALU.mult, ALU.add, [Rbank[tbk_], RBT[s]], [RSB[q]])
            e = er(); G["e"] = e
            act(ET[e], SB[q], AF.Exp, [RSB[q]], [RET[e]])

        def st2(G):
            h = G["h"]; s = h % 2; i = G["i"]; T = QT0 + i; o_ = G["tb"]; e = G["e"]
            for dd in range(5):
                kt = T - 4 + dd - KB0
                mm(bank[o_][:, 128:257], ET[e][:, dd * 128:(dd + 1) * 128], V[s][:, kt, 0:129], dd == 0, dd == 4,
                   [RET[e], RV[s]], [Rbank[o_]])

        def st3(G):
            o_ = G["tb"]
            f = fr(); G["f"] = f
            ts("dve", rr[f], bank[o_][:, 256:257], TINY, None, ALU.add, None, [Rbank[o_]], [Rrr[f]])
            recip(rr[f], rr[f], [Rrr[f]], [Rrr[f]])
            act(ob[f], bank[o_][:, 128:256], AF.Copy, [Rbank[o_], Rrr[f]], [Ro[f]], scale=rr[f][:, 0:1])

        def fin_tr(G):
            h = G["h"]; s = h % 2; i = G["i"]; f = G["f"]
            tb_ = G["tb"]
            tr(bank[tb_][:, 384:512], ob[f], ident_f, [Ro[f]], [Rbank[tb_]])
            cpy("act", oT[s][:, i * 128:(i + 1) * 128], bank[tb_][:, 384:512], [Rbank[tb_]], [RoT[s]])
            if i == NQT - 1:
                dma("sp", ATmix[WA + h * 128:WA + (h + 1) * 128, :], oT[s], "bo%d" % s, reads=[RoT[s]])

        load_head(0)
        if c.NBH > 1: load_head(1)
        LA = 2
        NG_ = len(groups)
        for n_ in range(min(LA, NG_)): qk(groups[n_])
        st1(groups[0])
        for n_, G in enumerate(groups):
            if n_ + LA < NG_: qk(groups[n_ + LA])
            if n_ + 1 < NG_: st1(groups[n_ + 1])
            st2(G)
            if n_ >= 1: st3(groups[n_ - 1])
            if n_ >= 2: fin_tr(groups[n_ - 2])
            if G["i"] == 2 and G["h"] >= 1 and G["h"] + 1 < c.NBH:
                load_head(G["h"] + 1)
        st3(groups[-1])
        if NG_ >= 2: fin_tr(groups[-2])
        fin_tr(groups[-1])

        P.barrier()
        A.release(m0)

    attn_B()

    if upto < 7:
        return finish()
    AT3 = alloc_AT(NQ)
    load_AT(ATmix, AT3, NQ)
    pre_x, epi_x = mk_epi_resid(xl, lambda j: (QT0 + j) * 128, X1, lambda j: j * 128, CBW)
    gemm([dict(form="T", Wv=wv(w_out), col0=0, ncols=D, tiles=list(range(NQT)), pre=pre_x, epi=epi_x)], AT3, KC, CBW, WSLOTS=3)
    A.release(PM)
    AT3 = alloc_AT(NQ)
    norm_phase([X1[i * 128:(i + 1) * 128, :] for i in range(NQT)], ncols[:, KC:2 * KC], AT3)
    _, epi_mq = mk_epi_plainF(QmT, lambda t0: t0, 0, "mq")
    gemm([dict(form="F", Wv=wv(w_mq), col0=0, ncols=D, toks=chunks(0, NQ), pre=nopre, epi=epi_mq)], AT3, KC, CBW, WSLOTS=3)
    A.release(PM)

    if upto < 8:
        return finish()
    def attn_M():
        m0 = A.mark()
        MC = c.MHD // 128
        NMT = MEM // 128
        Km = A.bf16(KC * MEM).rearrange("p (k n) -> p k n", n=MEM)
        Vv = A.bf16(NMT * D).rearrange("p (t d) -> p t d", d=D)
        Qc = [A.bf16(MC * 512).rearrange("p (k n) -> p k n", n=512) for _ in range(3)]
        ET = [A.bf16(NMT * 512).rearrange("p (t n) -> p t n", n=512) for _ in range(3)]
        Rl = [A.f32(512) for _ in range(2)]
        oS = [A.bf16(512) for _ in range(3)]
        RK = P.R(); RQ = P.Rs(3); RET = P.Rs(3); RRl = P.Rs(2); RoS = P.Rs(3)
        dma("sp", Km, KmT.rearrange("(k p) n -> p k n", p=128), "mk0", writes=[RK])
        dma("sp", Vv, Vm.rearrange("(t p) d -> p t d", p=128), "mk0", writes=[RK])
        scm = c.MHD ** -0.5
        qr = ring(3); sbr = ring(2); er = ring(3); obr = ring(3); osr = ring(3); lr = ring(2)
        QmTv = QmT.rearrange("(k p) n -> p k n", p=128)
        ATmov = ATmo.rearrange("(k p) n -> p k n", p=128)
        alt = ring(2)
        its = [dict(h=h, t0=t0, n=n) for h in range(c.NMH) for (t0, n) in chunks(0, NQ)]

        def loadQ(I):
            h, t0, n = I["h"], I["t0"], I["n"]
            qs = qr(); I["qs"] = qs
            dma("sp", Qc[qs][:, :, 0:n], QmTv[:, h * MC:(h + 1) * MC, t0:t0 + n], "mq%d" % qs, writes=[RQ[qs]])

        def part1(I):
            h, t0, n, qs = I["h"], I["t0"], I["n"], I["qs"]
            e = er(); I["e"] = e
            for mt in range(NMT):
                sb = sbr()
                for k in range(MC):
                    mm(bank[sb][:, 0:n], Km[:, h * MC + k, mt * 128:(mt + 1) * 128], Qc[qs][:, k, 0:n], k == 0, k == MC - 1,
                       [RK, RQ[qs]], [Rbank[sb]])
                act(ET[e][:, mt, 0:n], bank[sb][:, 0:n], AF.Exp, [Rbank[sb]], [RET[e]], scale=scm)

        def part2(I):
            h, t0, n, e = I["h"], I["t0"], I["n"], I["e"]
            lb = 2
            for mt in range(NMT):
                mm(bank[lb][:, 0:n], ones_b, ET[e][:, mt, 0:n], mt == 0, mt == NMT - 1, [RET[e]], [Rbank[lb]])
            l_ = lr()
            act(Rl[l_][:, 0:n], bank[lb][:, 0:n], AF.Ln, [Rbank[lb]], [RRl[l_]])
            act(Rl[l_][:, 0:n], Rl[l_][:, 0:n], AF.Exp, [RRl[l_]], [RRl[l_]], scale=-1.0)
            for k in range(MC):
                o_ = 3 + obr()
                for mt in range(NMT):
                    mm(bank[o_][:, 0:n], Vv[:, mt, (h * MC + k) * 128:(h * MC + k + 1) * 128], ET[e][:, mt, 0:n],
                       mt == 0, mt == NMT - 1, [RK, RET[e]], [Rbank[o_]])
                q = osr()
                tt("dve", oS[q][:, 0:n], bank[o_][:, 0:n], Rl[l_][:, 0:n], ALU.mult, [Rbank[o_], RRl[l_]], [RoS[q]])
                dma("sp", ATmov[:, h * MC + k, t0:t0 + n], oS[q][:, 0:n], "mo%d" % q, reads=[RoS[q]])

        for n_ in range(min(2, len(its))): loadQ(its[n_])
        part1(its[0])
        for n_, I in enumerate(its):
            if n_ + 2 < len(its): loadQ(its[n_ + 2])
            if n_ + 1 < len(its): part1(its[n_ + 1])
            part2(I)
        P.barrier()
        A.release(m0)

    attn_M()

    if upto < 9:
        return finish()
    AT3 = alloc_AT(NQ)
    load_AT(ATmo, AT3, NQ)
    pre_x, epi_x = mk_epi_resid(X1, lambda j: j * 128, X2, lambda j: j * 128, CBW)
    gemm([dict(form="T", Wv=wv(w_mo), col0=0, ncols=D, tiles=list(range(NQT)), pre=pre_x, epi=epi_x)], AT3, KC, CBW, WSLOTS=3)
    A.release(PM)

    if upto < 10:
        return finish()
    def ffn_up(hf):
        NT = TH + 1; NQh = NT * 128; NOh = TH * 128
        AT3 = alloc_AT(NQh)
        norm_phase([X2[(hf * TH + i) * 128:(hf * TH + i + 1) * 128, :] for i in range(NT)], ncols[:, 3 * KC:4 * KC], AT3)
        m0 = A.mark()
        DFF, NFB = c.DFF, c.NFB
        Wup = wv(w_up)
        wg = [A.bf16(KC * 128).rearrange("p (k c) -> p k c", c=128) for _ in range(2)]
        wu = [A.bf16(KC * 128).rearrange("p (k c) -> p k c", c=128) for _ in range(2)]
        G = [A.f32(NQh) for _ in range(2)]
        Cv = [A.f32(NOh) for _ in range(2)]
        Sg = [A.f32(NOh) for _ in range(2)]
        aT = [A.bf16(NOh) for _ in range(2)]
        cw = A.f32(NFB * 4)
        RW = P.Rs(2); RG = P.Rs(2); RC = P.Rs(2); RS = P.Rs(2); RaT = P.Rs(2); Rcw = P.R()
        dma("sp", cw, cw_d, "c0", writes=[Rcw])
        gch = [(126, 2)] + chunks(128, NQh); uch = chunks(128, NQh)
        for j in range(NFB):
            s = j % 2
            dma("pool", wg[s], Wup[:, :, j * 128:(j + 1) * 128], "ug%d" % s, writes=[RW[s]])
            dma("pool", wu[s], Wup[:, :, DFF + j * 128:DFF + (j + 1) * 128], "ug%d" % s, writes=[RW[s]])
            for (t0, n) in gch:
                b = gb()
                for kc in range(KC):
                    mm(bank[b][:, 0:n], wg[s][:, kc, :], AT3[:, kc, t0:t0 + n], kc == 0, kc == KC - 1, [RW[s]], [Rbank[b]])
                if t0 < 128:
                    act(G[s][:, t0:t0 + n], bank[b][:, 0:n], AF.Copy, [Rbank[b]], [RG[s]],
                        scale=(flag[:, 0:1] if hf == 0 else 1.0))
                else:
                    cpy("act", G[s][:, t0:t0 + n], bank[b][:, 0:n], [Rbank[b]], [RG[s]])
            w0 = cw[:, 4 * j:4 * j + 1]; w1 = cw[:, 4 * j + 1:4 * j + 2]; w2 = cw[:, 4 * j + 2:4 * j + 3]; cb = cw[:, 4 * j + 3:4 * j + 4]
            ts("dve", Cv[s], G[s][:, 128:NQh], w2, cb, ALU.mult, ALU.add, [RG[s], Rcw], [RC[s]])
            stt(Cv[s], G[s][:, 127:NQh - 1], w1, Cv[s], ALU.mult, ALU.add, [RG[s], RC[s]], [RC[s]])
            stt(Cv[s], G[s][:, 126:NQh - 2], w0, Cv[s], ALU.mult, ALU.add, [RG[s], RC[s]], [RC[s]])
            act(Sg[s], Cv[s], AF.Silu, [RC[s]], [RS[s]])
            for (t0, n) in uch:
                b = gb()
                for kc in range(KC):
                    mm(bank[b][:, 0:n], wu[s][:, kc, :], AT3[:, kc, t0:t0 + n], kc == 0, kc == KC - 1, [RW[s]], [Rbank[b]])
                tt("dve", aT[s][:, t0 - 128:t0 - 128 + n], bank[b][:, 0:n], Sg[s][:, t0 - 128:t0 - 128 + n], ALU.mult,
                   [Rbank[b], RS[s]], [RaT[s]])
            dma("sp", ACTT[j * 128:(j + 1) * 128, hf * NOh:(hf + 1) * NOh], aT[s], "ua%d" % s, reads=[RaT[s]])
        P.barrier()
        A.release(PM)

    ffn_up(0)
    ffn_up(1)

    if upto < 11:
        return finish()
    def ffn_down():
        m0 = A.mark()
        NFB = c.NFB
        GT = 4 if c.TO % 4 == 0 else 1
        NG = c.TO // GT
        KG = 4 if NFB >= 4 else NFB
        CB = 512 if D >= 512 else D
        NCB = D // CB
        Wd = wv(WDb)
        P.join_bg("pool")
        Ag = A.bf16(NFB * GT * 128).rearrange("p (k n) -> p k n", n=GT * 128)
        X3 = [A.f32(D) for _ in range(GT)]
        wp = [A.bf16(KG * CB).rearrange("p (k c) -> p k c", c=CB) for _ in range(3)]
        xs = [A.f32(CB) for _ in range(2)]
        fw = A.f32(D)
        junk = A.bf16(CB)
        ssq = [A.f32(NCB) for _ in range(GT)]; RSQ = P.Rs(GT); Rjk = P.R()
        ss = [A.f32(1) for _ in range(2)]; rs = [A.f32(1) for _ in range(2)]
        RA = P.Rs(4); RX3 = P.Rs(GT); Rwp = P.Rs(3); Rxs = P.Rs(2); Rfw = P.R(); Rss = P.Rs(2); Rrs = P.Rs(2)
        dma("sp", fw, fnw_d, "c0", writes=[Rfw])
        ACTv = ACTT.rearrange("(k p) n -> p k n", p=128)
        wpr = ring(3); xr = ring(2); fr = ring(2)
        pieces = [(k0, min(NFB, k0 + KG)) for k0 in range(0, NFB, KG)]
        qstep = (NFB + 3) // 4
        qof = lambda k: min(3, k // qstep)

        def load_Ag(g):
            for qi in range(4):
                k0 = qi * qstep; k1 = min(NFB, k0 + qstep)
                if k0 < k1:
                    dma("sp", Ag[:, k0:k1, :], ACTv[:, k0:k1, g * GT * 128:(g + 1) * GT * 128], "da%d" % qi, writes=[RA[qi]])
        load_Ag(0)
        for g in range(NG):
            for cb_ in range(NCB):
                c0 = cb_ * CB
                bs = [(cb_ % 2) * 4 + t for t in range(GT)]
                for (k0, k1) in pieces:
                    w_ = wpr()
                    dma("pool", wp[w_][:, 0:k1 - k0, :], Wd[:, k0:k1, c0:c0 + CB], "dw%d" % w_, writes=[Rwp[w_]])
                    for t in range(GT):
                        for k in range(k0, k1):
                            mm(bank[bs[t]][:, 0:CB], Ag[:, k, t * 128:(t + 1) * 128], wp[w_][:, k - k0, :], k == 0, k == NFB - 1,
                               [RA[qof(k)], Rwp[w_]], [Rbank[bs[t]]])
                for t in range(GT):
                    x_ = xr()
                    r0 = 128 + (g * GT + t) * 128
                    dma("sp", xs[x_], X2[r0:r0 + 128, c0:c0 + CB], "dx%d" % x_, writes=[Rxs[x_]])
                    tt("dve", X3[t][:, c0:c0 + CB], bank[bs[t]][:, 0:CB], xs[x_], ALU.add, [Rbank[bs[t]], Rxs[x_]], [RX3[t]])
                    act(junk, X3[t][:, c0:c0 + CB], AF.Square, [RX3[t]], [RSQ[t], Rjk], accum_out=ssq[t][:, cb_:cb_ + 1])
            if g + 1 < NG: load_Ag(g + 1)
            for t in range(GT):
                f = fr()
                P.emit("dve", (lambda t, f: lambda e: e.tensor_reduce(out=ss[f], in_=ssq[t], op=ALU.add,
                                                                      axis=mybir.AxisListType.X))(t, f), [RSQ[t]], [Rss[f]])
                rstd_chain(rs[f], ss[f], 1.0 / D, 1e-6, Rss[f], Rrs[f])
                stt(X3[t], X3[t], rs[f][:, 0:1], fw, ALU.mult, ALU.mult, [RX3[t], Rrs[f], Rfw], [RX3[t]])
                r0 = (g * GT + t) * 128
                dma("sp", out_d[r0:r0 + 128, :], X3[t], "do%d" % t, reads=[RX3[t]])
        P.barrier()
        A.release(m0)

    ffn_down()
    return finish()


def host_consts(cfg):
    c = cfg
    ident = np.eye(128, dtype=np.float32)
    Rm = np.zeros((128, 128), np.float32)
    for d in range(64):
        Rm[d + 64, d] = -1.0
        Rm[d, d + 64] = 1.0
    ones = np.ones((128, 128), np.float32)
    cst = np.concatenate([ident, Rm, ones], axis=1)
    return np.ascontiguousarray(cst)


def rope_tables_T(positions):
    inv = (1.0 / (np.float32(10000.0) ** (np.arange(0, 128, 2, dtype=np.float32) / np.float32(128)))).astype(np.float32)
    ang = positions.astype(np.float32)[:, None] * inv[None, :]
    ang = np.concatenate([ang, ang], axis=-1)
    return np.ascontiguousarray(np.cos(ang).astype(np.float32).T), np.ascontiguousarray(np.sin(ang).astype(np.float32).T)


def band_tables(cfg, rel_bias):
    i = np.arange(128)[:, None]; j = np.arange(128)[None, :]
    tabs = []
    for dd in range(5):
        rel = np.clip(128 * (dd - 4) + i - j, -256, 256) + 256
        tabs.append(rel)
    idx = np.stack(tabs, 0)
    g = rel_bias[:, idx]
    g = np.transpose(g, (2, 0, 1, 3))
    return np.ascontiguousarray(g.reshape(128, -1).astype(np.float32))


def col_layout(v):
    return np.ascontiguousarray(v.reshape(-1, 128).T)


def make_in_maps(cfg, x, mem, norm_mix_w, w_in, lambda_q1, lambda_k1, lambda_q2, lambda_k2, subln_w, rel_bias,
                 w_out, norm_xq_w, norm_mem_w, w_mq, w_mkv, w_mo, norm_ffn_w, w_up, conv_w, conv_b, w_down,
                 final_norm_w):
    c = cfg
    f = lambda a: np.ascontiguousarray(np.asarray(a, dtype=np.float32))
    x = f(x); mem = f(mem)
    S = x.shape[1]
    half = S // 2
    assert half == c.TO * 128 == c.TP * 128
    shared = {
        "w_in": f(w_in[0]), "w_out": f(w_out[0]), "w_mq": f(w_mq[0]), "w_mkv": f(w_mkv[0]), "w_mo": f(w_mo[0]),
        "w_up": f(w_up[0]), "w_down": f(w_down[0]),
        "ncols": np.ascontiguousarray(np.concatenate([col_layout(f(norm_mix_w[0])), col_layout(f(norm_xq_w[0])),
                                                      col_layout(f(norm_mem_w[0])), col_layout(f(norm_ffn_w[0]))], axis=1)),
        "fnw": np.ascontiguousarray(np.broadcast_to(f(final_norm_w)[None, :], (128, c.D))),
        "cst": host_consts(c),
        "lamp": np.ascontiguousarray(np.broadcast_to(
            np.concatenate([f(lambda_q1[0]), f(lambda_k1[0]), f(lambda_q2[0]), f(lambda_k2[0])])[None, :], (128, 512))),
        "sublnb": np.ascontiguousarray(np.broadcast_to(f(subln_w[0])[None, :], (128, 256))),
        "cw": np.ascontiguousarray(np.stack([col_layout(f(conv_w[0][0])), col_layout(f(conv_w[0][1])),
                                             col_layout(f(conv_w[0][2])), col_layout(f(conv_b[0]))], axis=2).reshape(128, -1)),
        "bt": band_tables(c, f(rel_bias[0])),
    }
    maps = []
    for b in range(x.shape[0]):
        for h in range(2):
            if h == 0:
                xl = np.concatenate([np.zeros((half, c.D), np.float32), x[b, :half]], axis=0)
                pos = np.concatenate([np.zeros(half), np.arange(half)])
            else:
                xl = x[b]
                pos = np.arange(S)
            cosT, sinT = rope_tables_T(pos)
            m = dict(shared)
            m.update({"xl": np.ascontiguousarray(xl), "memb": mem[b], "cosT": cosT, "sinT": sinT,
                      "flagc": np.full((128, 1), float(h), np.float32)})
            maps.append(m)
    return maps


_CACHE = {}


def kernel(**inputs):
    cfg = Cfg()
    if "nc" not in _CACHE:
        _CACHE["nc"] = build(cfg)[0]
    nc = _CACHE["nc"]
    maps = make_in_maps(cfg, **inputs)
    n = len(maps)
    res = run_bass_kernel_spmd(nc, maps, core_ids=list(range(n)))
    B = inputs["x"].shape[0]
    out = np.empty((B, 2 * cfg.NO, cfg.D), np.float32)
    for b in range(B):
        for h in range(2):
            out[b, h * cfg.NO:(h + 1) * cfg.NO] = np.asarray(res.results[2 * b + h]["out"], dtype=np.float32)
    return out
```
